# Optimizing a Trainium2 kernel written in Bass

```python
import jax, jax.numpy as jnp
from jax import lax
import numpy as np

D_MODEL = 1024
BATCH = 8
SEQ = 2048
DEPTH = 2
DEC_BATCH = 128
DEC_SEQ = 1
PAST_LEN = 16384
PAGE_SIZE = 128

N_EVEN = (DEPTH + 1) // 2
N_ODD = DEPTH // 2
GLA_HEADS = 4
GLA_DK = D_MODEL // 2 // GLA_HEADS
GLA_DV = D_MODEL // GLA_HEADS
GLA_QK_WIDTH = GLA_HEADS * GLA_DK
GLA_V_WIDTH = GLA_HEADS * GLA_DV
GATE_RANK = 16
GATE_TEMP = 16.0
GLA_CHUNK = 64
SCONV_WIDTH = D_MODEL
SCONV_K = 3
CCONV_WIDTH = D_MODEL
CCONV_K = 31
RMS_EPS = 1e-6
LN_EPS = 1e-5
EVEN_SPLITS = (GLA_QK_WIDTH, GLA_QK_WIDTH, GLA_V_WIDTH, GLA_V_WIDTH, GATE_RANK,
               SCONV_WIDTH, SCONV_WIDTH, SCONV_WIDTH, SCONV_WIDTH)
EVEN_PROJ = sum(EVEN_SPLITS)
EVEN_MIX_WIDTH = GLA_V_WIDTH + SCONV_WIDTH

kernel_name = 'hybrid_gla_shortconv_conformer_decode_step'


def _split(p, sizes):
    idx = np.cumsum(sizes)[:-1].tolist()
    return jnp.split(p, idx, axis=-1)


def rmsnorm(x, g):
    xf = x.astype(jnp.float32)
    y = xf * lax.rsqrt(jnp.mean(xf * xf, axis=-1, keepdims=True) + RMS_EPS)
    return (y * g.astype(jnp.float32)).astype(x.dtype)


def layernorm(x, g, b):
    xf = x.astype(jnp.float32)
    mu = jnp.mean(xf, axis=-1, keepdims=True)
    xc = xf - mu
    y = xc * lax.rsqrt(jnp.mean(xc * xc, axis=-1, keepdims=True) + LN_EPS)
    return (y * g.astype(jnp.float32) + b.astype(jnp.float32)).astype(x.dtype)


def causal_dwconv(u, buf, w):
    k_width = w.shape[0]
    u_full = jnp.concatenate([buf.astype(u.dtype), u], axis=1)
    y = lax.conv_general_dilated(u_full, w.astype(u.dtype)[:, None, :], (1,), 'VALID',
                                 dimension_numbers=('NWC', 'WIO', 'NWC'),
                                 feature_group_count=u.shape[-1])
    return y, u_full[:, -(k_width - 1):]


def gla_mix(q, k, v, log_a, s0):
    bsz, t_len, n_h, _ = q.shape
    dv = v.shape[-1]
    L = min(GLA_CHUNK, t_len)
    pad = (-t_len) % L
    n_blk = (t_len + pad) // L

    def blocks(t):
        t = jnp.pad(t.astype(jnp.float32), ((0, 0), (0, pad), (0, 0), (0, 0)))
        return t.reshape(bsz, n_blk, L, n_h, t.shape[-1]).transpose(0, 3, 1, 2, 4)

    q, k, v, log_a = blocks(q), blocks(k), blocks(v), blocks(log_a)
    b = jnp.cumsum(log_a, axis=3)
    b_last = b[:, :, :, -1:, :]
    qe = q * jnp.exp(b)
    ke = k * jnp.exp(-b)
    causal = jnp.tril(jnp.ones((L, L), dtype=bool))
    scores = jnp.where(causal, jnp.einsum('bhnld,bhnmd->bhnlm', qe, ke), 0.0)
    o_intra = jnp.einsum('bhnlm,bhnmv->bhnlv', scores, v)
    ds = jnp.einsum('bhnld,bhnlv->bhndv', k * jnp.exp(b_last - b), v)
    decay = jnp.exp(b_last[:, :, :, 0, :])

    def step(s, inp):
        dec, d = inp
        return dec[..., None] * s + d, s

    s_final, s_prev = lax.scan(step, s0.astype(jnp.float32),
                               (jnp.moveaxis(decay, 2, 0), jnp.moveaxis(ds, 2, 0)))
    s_prev = jnp.moveaxis(s_prev, 0, 2)
    o = o_intra + jnp.einsum('bhnld,bhndv->bhnlv', qe, s_prev)
    o = o.transpose(0, 2, 3, 1, 4).reshape(bsz, n_blk * L, n_h, dv)[:, :t_len]
    return o, s_final


def even_layer(h, s0, buf, w_in, w_gate_up, b_gate_up, gla_norm_g, w_sconv, w_out):
    bsz, t_len, _ = h.shape
    q, k, v, g, a_low, hb, gate_b, gate_c, z_b = _split(h @ w_in, EVEN_SPLITS)
    q = q.reshape(bsz, t_len, GLA_HEADS, GLA_DK) * (GLA_DK ** -0.5)
    k = k.reshape(bsz, t_len, GLA_HEADS, GLA_DK)
    v = v.reshape(bsz, t_len, GLA_HEADS, GLA_DV)
    log_a = jax.nn.log_sigmoid((a_low @ w_gate_up + b_gate_up).astype(jnp.float32)) / GATE_TEMP
    log_a = log_a.reshape(bsz, t_len, GLA_HEADS, GLA_DK)
    o, s_new = gla_mix(q, k, v, log_a, s0)
    o = rmsnorm(o, gla_norm_g).reshape(bsz, t_len, GLA_V_WIDTH).astype(h.dtype) * jax.nn.silu(g)
    y, new_buf = causal_dwconv(gate_c * hb, buf, w_sconv)
    y = gate_b * y * jax.nn.silu(z_b)
    out = jnp.concatenate([o, y], axis=-1) @ w_out
    return out, s_new.astype(s0.dtype), new_buf


def odd_layer(h, buf, w_in, b_in, w_dw, b_dw, ln_g, ln_b, w_out, b_out):
    a, a_gate, z = jnp.split(h @ w_in + b_in, 3, axis=-1)
    u = a * jax.nn.sigmoid(a_gate)
    y, new_buf = causal_dwconv(u, buf, w_dw)
    y = jax.nn.silu(layernorm(y + b_dw, ln_g, ln_b)) * jax.nn.silu(z)
    return y @ w_out + b_out, new_buf


def trunk(x, s_gla, buf_s, buf_c, norm_g, w_in_a, w_gate_up, b_gate_up, gla_norm_g, w_sconv,
          w_out_a, w_in_c, b_in_c, w_dwconv, b_dwconv, ln_g, ln_b, w_out_c, b_out_c, final_norm_g):
    new_gla, new_s, new_c = [], [], []
    for layer in range(DEPTH):
        i = layer // 2
        h = rmsnorm(x, norm_g[layer])
        if layer % 2 == 0:
            out, s, bs = even_layer(h, s_gla[i], buf_s[i], w_in_a[i], w_gate_up[i], b_gate_up[i],
                                    gla_norm_g[i], w_sconv[i], w_out_a[i])
            new_gla.append(s)
            new_s.append(bs)
        else:
            out, bc = odd_layer(h, buf_c[i], w_in_c[i], b_in_c[i], w_dwconv[i], b_dwconv[i],
                                ln_g[i], ln_b[i], w_out_c[i], b_out_c[i])
            new_c.append(bc)
        x = x + out
    return rmsnorm(x, final_norm_g), jnp.stack(new_gla), jnp.stack(new_s), jnp.stack(new_c)


def setup_inputs(seed: int = 0) -> dict:
    key = jax.random.key(seed)
    ks = jax.random.split(key, 24)

    def nrm(k, shape, s):
        return jax.random.normal(k, shape, jnp.float32) * s

    return {
        'x_prompt': nrm(ks[0], (BATCH, SEQ, D_MODEL), 1.0),
        'x_sample': nrm(ks[1], (DEC_BATCH, DEC_SEQ, D_MODEL), 1.0),
        'state_gla': nrm(ks[2], (N_EVEN, DEC_BATCH, GLA_HEADS, GLA_DK, GLA_DV), 1.0),
        'state_sconv': nrm(ks[3], (N_EVEN, DEC_BATCH, SCONV_K - 1, SCONV_WIDTH), 1.0),
        'state_cconv': nrm(ks[4], (N_ODD, DEC_BATCH, CCONV_K - 1, CCONV_WIDTH), 0.5),
        'norm_g': 1.0 + nrm(ks[5], (DEPTH, D_MODEL), 0.02),
        'w_in_a': nrm(ks[6], (N_EVEN, D_MODEL, EVEN_PROJ), D_MODEL ** -0.5),
        'w_gate_up': nrm(ks[7], (N_EVEN, GATE_RANK, GLA_QK_WIDTH), GATE_RANK ** -0.5),
        'b_gate_up': nrm(ks[8], (N_EVEN, GLA_QK_WIDTH), 0.1),
        'gla_norm_g': 1.0 + nrm(ks[9], (N_EVEN, GLA_DV), 0.02),
        'w_sconv': nrm(ks[10], (N_EVEN, SCONV_K, SCONV_WIDTH), SCONV_K ** -0.5),
        'w_out_a': nrm(ks[11], (N_EVEN, EVEN_MIX_WIDTH, D_MODEL), EVEN_MIX_WIDTH ** -0.5),
        'w_in_c': nrm(ks[12], (N_ODD, D_MODEL, 3 * CCONV_WIDTH), D_MODEL ** -0.5),
        'b_in_c': nrm(ks[13], (N_ODD, 3 * CCONV_WIDTH), 0.02),
        'w_dwconv': nrm(ks[14], (N_ODD, CCONV_K, CCONV_WIDTH), CCONV_K ** -0.5),
        'b_dwconv': nrm(ks[15], (N_ODD, CCONV_WIDTH), 0.02),
        'ln_g': 1.0 + nrm(ks[16], (N_ODD, CCONV_WIDTH), 0.02),
        'ln_b': nrm(ks[17], (N_ODD, CCONV_WIDTH), 0.02),
        'w_out_c': nrm(ks[18], (N_ODD, CCONV_WIDTH, D_MODEL), CCONV_WIDTH ** -0.5),
        'b_out_c': nrm(ks[19], (N_ODD, D_MODEL), 0.02),
        'final_norm_g': 1.0 + nrm(ks[20], (D_MODEL,), 0.02),
    }


def reference(x_prompt, x_sample, state_gla, state_sconv, state_cconv, norm_g, w_in_a, w_gate_up,
              b_gate_up, gla_norm_g, w_sconv, w_out_a, w_in_c, b_in_c, w_dwconv, b_dwconv, ln_g, ln_b,
              w_out_c, b_out_c, final_norm_g):
    weights = (norm_g, w_in_a, w_gate_up, b_gate_up, gla_norm_g, w_sconv, w_out_a, w_in_c, b_in_c,
               w_dwconv, b_dwconv, ln_g, ln_b, w_out_c, b_out_c, final_norm_g)
    bp = x_prompt.shape[0]
    dt = x_prompt.dtype
    zero_gla = jnp.zeros((N_EVEN, bp, GLA_HEADS, GLA_DK, GLA_DV), dt)
    zero_s = jnp.zeros((N_EVEN, bp, SCONV_K - 1, SCONV_WIDTH), dt)
    zero_c = jnp.zeros((N_ODD, bp, CCONV_K - 1, CCONV_WIDTH), dt)
    y_prompt, gla_p, sconv_p, cconv_p = trunk(x_prompt, zero_gla, zero_s, zero_c, *weights)
    y_sample, gla_s, sconv_s, cconv_s = trunk(x_sample, state_gla, state_sconv, state_cconv, *weights)
    return (y_prompt, y_sample, gla_p, sconv_p, cconv_p, gla_s, sconv_s, cconv_s)
```

```python
import numpy as np
import ml_dtypes
from contextlib import ExitStack
import concourse.bass as bass
import concourse.mybir as mybir
from concourse.bass_utils import run_bass_kernel_spmd

F32 = mybir.dt.float32
BF16 = mybir.dt.bfloat16
AF = mybir.ActivationFunctionType
ALU = mybir.AluOpType

SAME_ENGINE_SYNC = True
RMS_EPS = 1e-6
LN_EPS = 1e-5
NCORES = 8
NS = 16
TOK = 2048
STOP_AFTER = None


class Buf:
    def __init__(self, name, t):
        self.name = name
        self.t = t
        self.writes = {}
        self.reads = {}
        self.dsem = None
        self.dcnt = 0

    def __getitem__(self, idx):
        return View(self, self.t[idx])

    def all(self):
        return View(self, self.t[:])


class View:
    def __init__(self, buf, ap):
        self.buf = buf
        self.ap = ap

    def __getitem__(self, idx):
        return View(self.buf, self.ap[idx])

    def re(self, pat, **kw):
        return View(self.buf, self.ap.rearrange(pat, **kw))

    def bc(self, shape):
        return View(self.buf, self.ap.broadcast_to(list(shape)))

    def cast(self, dt):
        return View(self.buf, self.ap.bitcast(dt))


def _ap(v):
    return v.ap if isinstance(v, View) else v


class Ring:
    def __init__(self, bufs):
        self.bufs = bufs
        self.i = 0
        self.held = set()

    def get(self, hold=False):
        for _ in range(len(self.bufs) + 1):
            b = self.bufs[self.i]
            self.i = (self.i + 1) % len(self.bufs)
            if id(b) not in self.held:
                if hold:
                    self.held.add(id(b))
                return b
        raise RuntimeError("ring exhausted")

    def release(self, b):
        self.held.discard(id(b))


class Prog:
    ENG = ("pe", "dve", "act", "pool", "sp")

    def __init__(self, nc, es):
        self.nc = nc
        self.es = es
        self.eng = {"pe": nc.tensor, "dve": nc.vector, "act": nc.scalar, "pool": nc.gpsimd, "sp": nc.sync}
        self.semobj = {}
        self.cnt = {}
        for k in self.ENG:
            self.semobj[k] = es.enter_context(nc.semaphore("s_" + k))
            self.cnt[k] = 0
        self.dcnts = {}
        self.known = {k: {} for k in self.ENG}
        self.nbuf = 0
        self.ninstr = {k: 0 for k in self.ENG}

    def sbuf(self, name, shape, dt, es=None):
        self.nsb = getattr(self, "nsb", 0) + 1
        t = (es or self.es).enter_context(self.nc.sbuf_tensor("sb%d_%s" % (self.nsb, name), list(shape), dt))
        return Buf(name, t)

    def psum(self, name, shape, dt):
        t = self.es.enter_context(self.nc.psum_tensor(name, list(shape), dt))
        return Buf(name, t)

    def dram(self, name, ap):
        b = Buf(name, ap)
        b.is_dram = True
        return b

    def _emit_waits(self, e, deps):
        for k, v in deps.items():
            if self.known[e].get(k, 0) >= v:
                continue
            self.eng[e].wait_ge(self.semobj[k], v)
            self.ninstr[e] += 1
            self.known[e][k] = v

    def _deps(self, e, reads, writes):
        deps = {}

        def merge(d, skip_self):
            for k, v in d.items():
                if k == e and skip_self:
                    continue
                if deps.get(k, 0) < v:
                    deps[k] = v

        for r in reads:
            merge(r.buf.writes, not SAME_ENGINE_SYNC)
        skip_w = (e == "pe") or (not SAME_ENGINE_SYNC)
        for w in writes:
            merge(w.buf.writes, skip_w)
            merge(w.buf.reads, skip_w)
        return deps

    def _record(self, key, val, reads, writes):
        for r in reads:
            if r.buf.reads.get(key, 0) < val:
                r.buf.reads[key] = val
        for w in writes:
            if w.buf.reads:
                w.buf.reads = {}
                w.buf.writes = {}
            w.buf.writes[key] = val

    def op(self, e, fn, reads=(), writes=(), inc=True):
        reads = [r for r in reads if isinstance(r, View)]
        writes = [w for w in writes if isinstance(w, View)]
        self._emit_waits(e, self._deps(e, reads, writes))
        ins = fn()
        self.ninstr[e] += 1
        if inc:
            self.cnt[e] += 1
            ins.then_inc(self.semobj[e], 1)
            val = self.cnt[e]
        else:
            val = self.cnt[e] + 1
        self._record(e, val, reads, writes)
        return ins

    def dma(self, q, out, in_, **kw):
        reads = [in_]
        writes = [out]
        self._emit_waits(q, self._deps(q, reads, writes))
        owner = out.buf
        if getattr(out.buf, "is_dram", False) and not getattr(in_.buf, "is_dram", False):
            owner = in_.buf
        kind = "sw" if q == "pool" else "hw"
        if not hasattr(owner, "dsems"):
            owner.dsems = {}
            owner.dcnts_ = {}
        if kind not in owner.dsems:
            nm = "d%d%s_%s" % (self.nbuf, kind[0], owner.name)
            self.nbuf += 1
            owner.dsems[kind] = nm
            owner.dcnts_[kind] = 0
            self.semobj[nm] = self.es.enter_context(self.nc.semaphore(nm))
        nm = owner.dsems[kind]
        ins = self.eng[q].dma_start(out=out.ap, in_=in_.ap, **kw)
        self.ninstr[q] += 1
        owner.dcnts_[kind] += 16
        self.dcnts[nm] = owner.dcnts_[kind]
        ins.then_inc(self.semobj[nm], 16)
        self._record(nm, owner.dcnts_[kind], reads, writes)
        return ins

    def barrier(self):
        deps = {}
        for k in self.ENG:
            if self.cnt[k] > 0:
                deps[k] = self.cnt[k]
        deps.update(self.dcnts)
        for e in self.ENG:
            self._emit_waits(e, dict(deps))

    def mm(self, out, lhsT, rhs, start=True, stop=True, inc=None):
        if inc is None:
            inc = stop
        return self.op("pe", lambda: self.nc.tensor.matmul(_ap(out), _ap(lhsT), _ap(rhs), start=start, stop=stop),
                       reads=[lhsT, rhs], writes=[out], inc=inc)

    def tr(self, out, in_, ident, inc=True):
        return self.op("pe", lambda: self.nc.tensor.transpose(_ap(out), _ap(in_), _ap(ident)),
                       reads=[in_, ident], writes=[out], inc=inc)

    def act(self, out, in_, func, scale=1.0, bias=None, accum_out=None):
        reads = [in_, scale, bias]
        writes = [out, accum_out]
        kw = {}
        if bias is not None:
            kw["bias"] = _ap(bias)
        if accum_out is not None:
            kw["accum_out"] = _ap(accum_out)
        return self.op("act", lambda: self.nc.scalar.activation(out=_ap(out), in_=_ap(in_), func=func,
                                                                scale=_ap(scale), **kw),
                       reads=reads, writes=writes)

    def ts(self, e, out, in0, s1, op0, s2=None, op1=None):
        kw = {}
        if op1 is not None:
            kw["op1"] = op1
        return self.op(e, lambda: self.eng[e].tensor_scalar(out=_ap(out), in0=_ap(in0), scalar1=_ap(s1),
                                                            scalar2=_ap(s2), op0=op0, **kw),
                       reads=[in0, s1, s2], writes=[out])

    def tt(self, e, out, in0, in1, op):
        return self.op(e, lambda: self.eng[e].tensor_tensor(out=_ap(out), in0=_ap(in0), in1=_ap(in1), op=op),
                       reads=[in0, in1], writes=[out])

    def stt(self, out, in0, scalar, in1, op0, op1):
        return self.op("dve", lambda: self.nc.vector.scalar_tensor_tensor(out=_ap(out), in0=_ap(in0),
                                                                        scalar=_ap(scalar), in1=_ap(in1),
                                                                        op0=op0, op1=op1),
                       reads=[in0, in1, scalar], writes=[out])

    def copy(self, e, out, in_):
        if e == "act":
            return self.op("act", lambda: self.nc.scalar.copy(out=_ap(out), in_=_ap(in_)), reads=[in_], writes=[out])
        return self.op(e, lambda: self.eng[e].tensor_copy(out=_ap(out), in_=_ap(in_)), reads=[in_], writes=[out])

    def memset(self, e, out, val):
        return self.op(e, lambda: self.eng[e].memset(_ap(out), val), reads=[], writes=[out])


IN_SPECS = [
    ("x", [TOK, 1024], F32), ("xs", [NS, 1024], F32), ("sgla", [NS, 4, 128, 256], F32),
    ("ssc", [NS, 2, 1024], F32), ("scc", [NS, 30, 1024], F32),
    ("w_in_a", [1024, 7184], F32), ("w_out_a", [2048, 1024], F32),
    ("w_in_c", [1024, 3072], F32), ("w_out_c", [1024, 1024], F32),
    ("wgu", [17, 512], F32), ("g0T", [128, 8], F32), ("g1T", [128, 8], F32), ("glaT", [128, 8], F32),
    ("wsT", [128, 8, 3], F32), ("wdT", [128, 8, 31], F32), ("bdw", [128, 8], F32),
    ("lng", [128, 8], F32), ("lnb", [128, 8], F32), ("binc", [128, 24], F32),
    ("bout", [1, 1024], F32), ("bincrow", [1, 3072], F32), ("fgbc", [128, 1024], F32),
    ("c_identb", [128, 128], BF16), ("c_identf", [128, 128], F32), ("c_trif", [128, 128], F32),
    ("c_delta", [128, 16, 16], F32),
]
OUT_SPECS = [
    ("y", [TOK, 1024], F32), ("ys", [NS, 1024], F32), ("glap", [4, 128, 256], F32),
    ("scp", [2, 1024], F32), ("ccp", [30, 1024], F32), ("glas", [NS, 4, 128, 256], F32),
    ("scs", [NS, 2, 1024], F32), ("ccs", [NS, 30, 1024], F32),
]


def build_nc(stop_after=None, debug_resid=False):
    nc = bass.Bass("TRN2", target_bir_lowering=False)
    D = {}
    with ExitStack() as es:
        P = Prog(nc, es)
        for name, shape, dt in IN_SPECS:
            D[name] = P.dram(name, nc.dram_tensor(name, shape, dt, kind="ExternalInput").ap())
        for name, shape, dt in OUT_SPECS:
            D[name] = P.dram(name, nc.dram_tensor(name, shape, dt, kind="ExternalOutput").ap())
        if debug_resid:
            D["dbg"] = P.dram("dbg", nc.dram_tensor("dbg", [17, 128, 1024], F32, kind="ExternalOutput").ap())
        D["dgd"] = P.dram("dgd", nc.dram_tensor("dgd", [8, 128, 31 * 128], BF16, kind="Internal").ap())
        wscd_t = nc.dram_tensor("wscd", [8, 128, 4096], BF16, kind="Internal").ap()
        wicd_t = nc.dram_tensor("wicd", [8, 128, 3072], BF16, kind="Internal").ap()
        wscd_b = P.dram("wscd", wscd_t)
        wicd_b = P.dram("wicd", wicd_t)
        WSCD = [wscd_b[c] for c in range(8)]
        WICD = [wicd_b[c] for c in range(8)]
        WOBD = P.dram("wobd", nc.dram_tensor("wobd", [128, 8192], BF16, kind="Internal").ap())
        WOCD = P.dram("wocd", nc.dram_tensor("wocd", [128, 8192], BF16, kind="Internal").ap())

        def precast_B():
            wsrc = D["w_in_a"][:, 3088:7184].re("(kc p) (g c n) -> p kc g c n", p=128, g=4, c=8)
            for c in range(8):
                dst = WSCD[c].re("p (kc g n) -> p kc g n", kc=8, g=4)
                for g in range(4):
                    P.dma("pool", dst[:, :, g, :], wsrc[:, :, g, c, :])
            dst = WOBD.all().re("p (kc n) -> p kc n", kc=8)
            for h in range(2):
                P.dma("pool", dst[:, :, h * 512:(h + 1) * 512],
                      D["w_out_a"][1024:2048, h * 512:(h + 1) * 512].re("(kc p) n -> p kc n", p=128))

        def precast_C():
            wsrc = D["w_in_c"].all().re("(kc p) (g c n) -> p kc g c n", p=128, g=3, c=8)
            for c in range(8):
                dst = WICD[c].re("p (kc g n) -> p kc g n", kc=8, g=3)
                for g in range(3):
                    P.dma("pool", dst[:, :, g, :], wsrc[:, :, g, c, :])
            dst = WOCD.all().re("p (kc n) -> p kc n", kc=8)
            for h in range(2):
                P.dma("pool", dst[:, :, h * 512:(h + 1) * 512],
                      D["w_out_c"][:, h * 512:(h + 1) * 512].re("(kc p) n -> p kc n", p=128))

        resid = [P.sbuf("resid%d" % i, [128, 1024], F32) for i in range(17)]
        identb = P.sbuf("identb", [128, 128], BF16)
        identf = P.sbuf("identf", [128, 128], F32)
        trif = P.sbuf("trif", [128, 128], F32)
        trib = P.sbuf("trib", [128, 128], BF16)
        smalls = {}
        for nm, shp in [("g0T", [128, 8]), ("g1T", [128, 8]), ("glaT", [128, 8]), ("wsT", [128, 8, 3]),
                        ("wdT", [128, 8, 31]), ("bdw", [128, 8]), ("lng", [128, 8]), ("lnb", [128, 8]),
                        ("binc", [128, 24])]:
            smalls[nm] = P.sbuf("c_" + nm, shp, F32)
            P.dma("sp", smalls[nm].all(), D[nm].all())
        ssq1 = P.sbuf("ssq1", [128, 17], F32)
        ssq1h = P.sbuf("ssq1h", [128, 17, 2], F32)
        rstd1 = P.sbuf("rstd1", [128, 17], F32)
        onesb = P.sbuf("onesb", [128, 128], BF16)
        onesdiv = P.sbuf("onesdiv", [128, 128], BF16)
        P.dma("sp", identb.all(), D["c_identb"].all())
        P.dma("sp", identf.all(), D["c_identf"].all())
        P.dma("sp", trif.all(), D["c_trif"].all())
        P.copy("dve", trib.all(), trif.all())
        trib16 = P.sbuf("trib16", [128, 128], BF16)
        neg16 = P.sbuf("neg16", [128, 1], BF16)
        P.ts("dve", trib16.all(), trif.all(), -1.0 / 16, ALU.mult)
        P.memset("dve", neg16.all(), -1.0 / 16)
        P.memset("dve", onesb.all(), 1.0)
        P.memset("dve", onesdiv.all(), 1.0 / 1024)
        P.memset("dve", ssq1h.all(), 1.0)

        banks = [P.psum("ps%d" % i, [128, 512], F32) for i in range(8)]
        ring = Ring(banks)

        def load_w(dst, src_view, ncols, q="pool"):
            c = 0
            while c < ncols:
                w = min(1024, ncols - c)
                P.dma(q, dst[:, :, c:c + w], src_view[:, c:c + w].re("(kc p) n -> p kc n", p=128))
                c += w

        def scale_rows(dst, gT, ncols, e="dve"):
            for kc in range(8):
                P.ts(e, dst[:, kc, 0:ncols], dst[:, kc, 0:ncols], gT[:, kc:kc + 1], ALU.mult)

        def rstd_from_ssq(dst, src, n, eps):
            P.act(dst, src, AF.Ln, scale=1.0 / n, bias=eps_t[eps][: dst.ap.shape[0], 0:1])
            P.act(dst, dst, AF.Exp, scale=-0.5)

        eps_t = {}
        for ev in (RMS_EPS, LN_EPS, 1.0):
            eps_t[ev] = P.sbuf("eps%g" % ev, [128, 1], F32)
            P.memset("dve", eps_t[ev].all(), float(ev))

        def transposes_to(dst_view_fn, src, ntok, nchunks, ident, dt_bf=True, e="act", gT=None):
            pb = ring.get()
            pv = pb.all().cast(BF16) if dt_bf else pb.all()
            for kc in range(nchunks):
                P.tr(pv[:, kc * ntok:(kc + 1) * ntok], src[:ntok, kc * 128:(kc + 1) * 128], ident[:ntok, :ntok],
                     inc=(kc == nchunks - 1))
            if gT is None:
                P.copy(e, dst_view_fn(), pv[:, 0:nchunks * ntok].re("p (k t) -> p k t", k=nchunks))
            else:
                dst = dst_view_fn()
                for kc in range(nchunks):
                    src_kc = pv[:, kc * ntok:(kc + 1) * ntok]
                    if e == "mix" and kc % 2 == 0:
                        P.act(dst[:, kc, :], src_kc, AF.Copy, scale=gT[:, kc:kc + 1])
                    else:
                        P.ts("dve", dst[:, kc, :], src_kc, gT[:, kc:kc + 1], ALU.mult)

        es_hT = ExitStack()
        hT = P.sbuf("hT", [128, 8, TOK + NS], BF16, es_hT)
        with ExitStack() as esA:
            WG = {}
            for nm, c0, w in (("a", 3072, 16), ("q", 0, 512), ("k", 512, 512), ("v0", 1024, 512), ("v1", 1536, 512),
                              ("g0", 2048, 512), ("g1", 2560, 512)):
                WG[nm] = (P.sbuf("Wg_" + nm, [128, 8, w], BF16, esA), c0, w)
            Wot = [P.sbuf("Wot%d" % n, [128, 8, 512], BF16, esA) for n in range(2)]
            wgu = P.sbuf("wgu", [32, 512], BF16, esA)
            delta = P.sbuf("delta", [128, 16, 16], F32, esA)
            P.dma("sp", delta.all(), D["c_delta"].all())
            P.dma("pool", wgu[0:17, :], D["wgu"].all())
            for nm in ("a", "q", "k", "v0", "v1", "g0", "g1"):
                wb, c0, w = WG[nm]
                load_w(wb, D["w_in_a"][:, c0:c0 + w], w)
            for n in range(2):
                load_w(Wot[n], D["w_out_a"][0:1024, n * 512:(n + 1) * 512], 512)
                scale_rows(Wot[n], smalls["glaT"], 512)

            ssqs = [P.sbuf("ssq%d" % i, [128, 1], F32, esA) for i in range(2)]
            rstds = [P.sbuf("rstd%d" % i, [128, 1], F32, esA) for i in range(2)]
            om = P.sbuf("om", [128, 1024], BF16, esA)
            omTf = P.sbuf("omTf", [128, 1024], BF16, esA)
            omT = omTf.all().re("p (k t) -> p k t", k=8)
            xns = [om, omTf]
            aaug = P.sbuf("aaug", [32, 128], BF16, esA)
            eb = P.sbuf("eb", [128, 512], F32, esA)
            qe = P.sbuf("qe", [128, 512], BF16, esA)
            sgh = P.sbuf("sgh", [128, 512], F32, esA)
            ossq = P.sbuf("ossq", [128, 4], F32, esA)
            rso = P.sbuf("rso", [128, 4], F32, esA)
            atok = eb
            qtok = qe
            SB = {"aT": P.sbuf("aT", [128, 4, 16], F32, esA), "qT": P.sbuf("qT", [128, 4, 16], BF16, esA),
                  "Qm": P.sbuf("Qm", [128, 4, 16, 16], BF16, esA), "vbfS": P.sbuf("vbfS", [128, 1024], BF16, esA),
                  "ksb": P.sbuf("ksb", [16, 512], F32, esA)}
            esAp = ExitStack()
            la = P.sbuf("la", [128, 512], BF16, esAp)
            enb = P.sbuf("enb", [128, 512], F32, esAp)
            scm = P.sbuf("scm", [128, 4, 128], BF16, esAp)
            S = P.sbuf("S", [128, 4, 256], F32, esAp)
            Sbf = P.sbuf("Sbf", [128, 4, 256], BF16, esAp)
            FBs = [{"qkT": P.sbuf("qkT%d" % i, [128, 8, 128], BF16, esAp), "ke": P.sbuf("ke%d" % i, [128, 512], BF16, esAp),
                    "vbf": P.sbuf("vbf%d" % i, [128, 1024], BF16, esAp), "dec": P.sbuf("dec%d" % i, [128, 4], F32, esAp)}
                   for i in range(2)]

            P.memset("dve", aaug.all(), 1.0)
            P.memset("dve", S.all(), 0.0)

            def tinfo(t):
                sample = (t == 16)
                return sample, (NS if sample else 128), (TOK if sample else t * 128)

            def proj_tok(ps, col0, ntok, nm):
                wb, _, ncols = WG[nm]
                for kc in range(8):
                    P.mm(ps[:ntok, :ncols], hT[:, kc, col0:col0 + ntok], wb[:, kc, :],
                         start=(kc == 0), stop=(kc == 7))

            for t in range(17):
                if t == 16:
                    P.dma("sp", resid[16][:NS, :], D["xs"].all())
                else:
                    P.dma("sp", resid[t].all(), D["x"][t * 128:(t + 1) * 128, :])
            for t in range(17):
                sample, ntok, col0 = tinfo(t)
                xt, xn, ssq, rstd = resid[t], xns[t % 2], ssqs[t % 2], rstds[t % 2]
                P.act(xn[:ntok, :], xt[:ntok, :], AF.Square, accum_out=ssq[:ntok, :])
                rstd_from_ssq(rstd[:ntok, :], ssq[:ntok, :], 1024, RMS_EPS)
                P.ts("dve", xn[:ntok, :], xt[:ntok, :], rstd[:ntok, :], ALU.mult)
                transposes_to(lambda: hT[:, :, col0:col0 + ntok], xn, ntok, 8, identb, e="mix", gT=smalls["g0T"])

            FR = {}

            def front(t):
                sample, ntok, col0 = tinfo(t)
                fb = FBs[t % 2]
                qkT, ke, vbf, dec = fb["qkT"], fb["ke"], fb["vbf"], fb["dec"]
                pa = ring.get()
                for kc in range(8):
                    P.mm(pa[:16, :ntok], WG["a"][0][:, kc, :], hT[:, kc, col0:col0 + ntok],
                         start=(kc == 0), stop=(kc == 7))
                P.copy("dve", aaug[0:16, :ntok], pa[:16, :ntok])
                yield 1.2
                pla = ring.get()
                P.mm(pla[:ntok, :], aaug[0:17, :ntok], wgu[0:17, :])
                pe1 = ring.get()
                P.act(pe1[:ntok, :], pla[:ntok, :], AF.Exp, scale=-1.0)
                P.act(la.all(), pe1.all(), AF.Ln, bias=eps_t[1.0][:, 0:1])
                yield 2.0
                pb = ring.get()
                P.mm(pb.all(), trib16.all(), la.all())
                pbl = ring.get()
                for h in range(4):
                    P.mm(pbl[:, h:h + 1], la[:, h * 128:(h + 1) * 128], neg16[:, 0:1], inc=(h == 3))
                P.act(eb.all(), pb.all(), AF.Exp)
                P.act(enb.all(), pb.all(), AF.Exp, scale=-1.0)
                P.act(dec.all(), pbl[:, 0:4], AF.Exp)
                yield 2.2
                pq = ring.get()
                proj_tok(pq, col0, ntok, "q")
                P.stt(qe.all(), pq.all(), 128 ** -0.5, eb.all(), ALU.mult, ALU.mult)
                yield 2.6
                pk = ring.get()
                proj_tok(pk, col0, ntok, "k")
                P.tt("dve", ke.all(), pk.all(), enb.all(), ALU.mult)
                yield 2.6
                tq = ring.get()
                tqv = tq.all().cast(BF16)
                for h in range(4):
                    P.tr(tqv[:, h * 128:(h + 1) * 128], qe[:, h * 128:(h + 1) * 128], identb.all(), inc=False)
                for h in range(4):
                    P.tr(tqv[:, 512 + h * 128:512 + (h + 1) * 128], ke[:, h * 128:(h + 1) * 128], identb.all(),
                         inc=(h == 3))
                P.copy("dve", qkT.all().re("p k t -> p (k t)"), tqv)
                yield 1.8
                for n in range(2):
                    pv = ring.get()
                    proj_tok(pv, col0, ntok, "v%d" % n)
                    P.copy("act", vbf[:ntok, n * 512:(n + 1) * 512], pv[:ntok, :])
                    yield 2.5

            def front_sample():
                t = 16
                sample, ntok, col0 = tinfo(t)
                vbf = SB["vbfS"]
                aT, qT, Qm = SB["aT"], SB["qT"], SB["Qm"]
                pa = ring.get()
                for kc in range(8):
                    P.mm(pa[:16, :ntok], WG["a"][0][:, kc, :], hT[:, kc, col0:col0 + ntok],
                         start=(kc == 0), stop=(kc == 7))
                P.copy("dve", aaug[0:16, :ntok], pa[:16, :ntok])
                pla = ring.get()
                P.mm(pla[:ntok, :], aaug[0:17, :ntok], wgu[0:17, :])
                pe1 = ring.get()
                P.act(pe1[:ntok, :], pla[:ntok, :], AF.Exp, scale=-1.0)
                pl1 = ring.get()
                P.act(pl1[:ntok, :], pe1[:ntok, :], AF.Ln, bias=eps_t[1.0][:ntok, 0:1])
                P.act(atok[:NS, :], pl1[:NS, :], AF.Exp, scale=-1.0 / 16)
                pq = ring.get()
                proj_tok(pq, col0, ntok, "q")
                P.ts("dve", qtok[:NS, :], pq[:NS, :], 128 ** -0.5, ALU.mult)
                yield 4.0
                pk = ring.get()
                proj_tok(pk, col0, ntok, "k")
                P.copy("dve", SB["ksb"].all(), pk[:NS, :])
                yield 2.5
                for n in range(2):
                    pv = ring.get()
                    proj_tok(pv, col0, ntok, "v%d" % n)
                    P.copy("act", vbf[:ntok, n * 512:(n + 1) * 512], pv[:ntok, :])
                    yield 2.5
                pt = ring.get()
                for h in range(4):
                    P.tr(pt[:, h * 16:(h + 1) * 16], atok[:NS, h * 128:(h + 1) * 128], identf[:16, :16], inc=(h == 3))
                P.copy("act", aT.all().re("p h s -> p (h s)"), pt[:, 0:64])
                pt2 = ring.get()
                pt2v = pt2.all().cast(BF16)
                for h in range(4):
                    P.tr(pt2v[:, h * 16:(h + 1) * 16], qtok[:NS, h * 128:(h + 1) * 128], identb[:16, :16], inc=(h == 3))
                P.copy("act", qT.all().re("p h s -> p (h s)"), pt2v[:, 0:64])
                yield 2.0

            def core(t):
                fb = FBs[t % 2]
                qkT, ke, vbf, dec = fb["qkT"], fb["ke"], fb["vbf"], fb["dec"]
                psc = ring.get()
                for h in range(4):
                    P.mm(psc[:, h * 128:(h + 1) * 128], qkT[:, 4 + h, :], qkT[:, h, :], inc=(h == 3))
                P.tt("dve", scm.all(), psc.all().re("p (h l) -> p h l", h=4),
                     trif.all().re("p (o l) -> p o l", o=1).bc([128, 4, 128]), ALU.mult)
                yield 1.5
                po = [ring.get(hold=True), ring.get(hold=True)]
                pos = []
                for h in range(4):
                    ov = po[h // 2][:, (h % 2) * 256:(h % 2 + 1) * 256]
                    P.mm(ov, scm[:, h, :], vbf[:, h * 256:(h + 1) * 256], start=True, stop=(t == 0))
                    if t > 0:
                        P.mm(ov, qkT[:, h, :], Sbf[:, h, :], start=False, stop=True)
                    pos.append(ov)
                FR["pos"] = pos
                yield 1.2
                for h2 in range(2):
                    pd = ring.get()
                    for hh in range(2):
                        h = h2 * 2 + hh
                        P.mm(pd[:, hh * 256:(hh + 1) * 256], ke[:, h * 128:(h + 1) * 128],
                             vbf[:, h * 256:(h + 1) * 256], inc=(hh == 1))
                    for hh in range(2):
                        h = h2 * 2 + hh
                        P.tt("dve", S[:, h, :], pd[:, hh * 256:(hh + 1) * 256], S[:, h, :], ALU.add)
                        P.act(S[:, h, :], S[:, h, :], AF.Copy, scale=dec[:, h:h + 1])
                        if t < 15:
                            P.copy("dve", Sbf[:, h, :], S[:, h, :])
                    yield 1.5
                if t == 15:
                    P.dma("sp", D["glap"].all().re("h d v -> d h v"), S.all())

            def core_sample():
                pk = SB["ksb"]
                vbf = SB["vbfS"]
                aT, qT, Qm, kms, s0b, snbf = SB["aT"], SB["qT"], SB["Qm"], SB["kms"], SB["s0b"], SB["snbf"]
                qaT, qkp, qk, osb = SB["qaT"], SB["qkp"], SB["qk"], SB["osb"]
                P.tt("dve", qaT.all(), qT.all(), aT.all(), ALU.mult)
                for h in range(4):
                    P.tt("dve", Qm[:, h, :, :], qaT[:, h, :].re("p (s o) -> p s o", o=1).bc([128, 16, 16]),
                         delta.all(), ALU.mult)
                P.tt("dve", qkp, qtok[:NS, :], pk[:NS, :], ALU.mult)
                P.op("dve", lambda: nc.vector.tensor_reduce(out=qk.all().ap, in_=qkp.re("s (h d) -> s h d", h=4).ap,
                                                            axis=mybir.AxisListType.X, op=ALU.add),
                     reads=[qkp], writes=[qk.all()])
                def load_s0(si):
                    P.dma("sp", s0b[si % 4].all(), D["sgla"][si].re("h d v -> d h v"))

                for si in range(3):
                    load_s0(si)
                pobs = [ring.get(hold=True) for _ in range(4)]
                kms2 = SB["kms2"]
                stg = {}

                def stage1(si):
                    km = kms2[0]
                    P.ts("dve", km.all(), pk[:NS, :], identf[:NS, si:si + 1], ALU.mult)
                    pds = [ring.get(), ring.get()]
                    for h in range(4):
                        P.mm(pds[h // 2][:, (h % 2) * 256:(h % 2 + 1) * 256], km[:, h * 128:(h + 1) * 128],
                             vbf[:NS, h * 256:(h + 1) * 256])
                    stg[si] = pds

                stage1(0)
                for si in range(NS):
                    if si + 3 < NS:
                        load_s0(si + 3)
                    if si + 1 < NS:
                        stage1(si + 1)
                    pds = stg.pop(si)
                    s0 = s0b[si % 4]
                    sn = s0
                    s0f = snbf[si % 2]
                    P.copy("act", s0f.all(), s0.all())
                    for h in range(4):
                        P.stt(sn[:, h, :], s0[:, h, :], aT[:, h, si:si + 1],
                              pds[h // 2][:, (h % 2) * 256:(h % 2 + 1) * 256], ALU.mult, ALU.add)
                        P.mm(pobs[h][:NS, 0:256], Qm[:, h, si, :], s0f[:, h, :], start=(si == 0), stop=(si == NS - 1))
                    P.dma("pool", D["glas"][si].re("h d v -> d h v"), sn.all())
                pos = []
                for h in range(4):
                    P.stt(osb[:, h * 256:(h + 1) * 256], vbf[:NS, h * 256:(h + 1) * 256], qk[:, h:h + 1],
                          pobs[h][:NS, 0:256], ALU.mult, ALU.add)
                    ring.release(pobs[h])
                    pos.append(osb[:, h * 256:(h + 1) * 256])
                return pos

            def tail(t, pos):
                sample, ntok, col0 = tinfo(t)
                xt = resid[t]
                pj = ring.get()
                for h in range(4):
                    P.act(pj[:ntok, 0:256], pos[h][:ntok, :], AF.Square, accum_out=ossq[:ntok, h:h + 1])
                rstd_from_ssq(rso[:ntok, :], ossq[:ntok, :], 256, RMS_EPS)
                for n in range(2):
                    pg = ring.get()
                    proj_tok(pg, col0, ntok, "g%d" % n)
                    P.act(sgh[:ntok, :], pg[:ntok, :], AF.Silu)
                    for hh in range(2):
                        h = n * 2 + hh
                        P.stt(om[:ntok, h * 256:(h + 1) * 256], pos[h][:ntok, :], rso[:ntok, h:h + 1],
                              sgh[:ntok, hh * 256:(hh + 1) * 256], ALU.mult, ALU.mult)
                    yield 3.0
                for b_ in set(id(v.buf) for v in pos):
                    ring.held.discard(b_)
                transposes_to(lambda: omT[:, :, :ntok], om, ntok, 8, identb, e="dve")
                yield 1.5
                for n in range(2):
                    pout = ring.get()
                    for kc in range(8):
                        P.mm(pout[:ntok, :], omT[:, kc, :ntok], Wot[n][:, kc, :],
                             start=(kc == 0), stop=(kc == 7))
                    P.tt("dve", xt[:ntok, n * 512:(n + 1) * 512], pout[:ntok, :],
                         xt[:ntok, n * 512:(n + 1) * 512], ALU.add)
                    yield 2.5

            def merge(*gens, skew=()):
                gens = [g for g in gens if g is not None]
                clk = [0.0] * len(gens)
                for i, v in enumerate(skew):
                    if i < len(clk):
                        clk[i] = v
                alive = list(range(len(gens)))
                while alive:
                    i = min(alive, key=lambda k: clk[k])
                    try:
                        dt = next(gens[i])
                        clk[i] += dt if dt else 1.0
                    except StopIteration:
                        alive.remove(i)

            def body(t):
                yield from core(t)
                yield from tail(t, FR["pos"])

            merge(front(0))
            for t in range(16):
                if t == 1:
                    precast_B()
                if t == 5:
                    precast_C()
                merge(front(t + 1) if t + 1 < 16 else front_sample(), body(t))
            P.barrier()
            esAp.close()
            with ExitStack() as esAs:
                SB.update({"kms": None, "kms2": [P.sbuf("kms2%d" % i, [16, 512], BF16, esAs) for i in range(1)],
                           "s0b": [P.sbuf("s0b%d" % i, [128, 4, 256], F32, esAs) for i in range(4)],
                           "snbf": [P.sbuf("snbf%d" % i, [128, 4, 256], BF16, esAs) for i in range(2)],
                           "qaT": P.sbuf("qaT", [128, 4, 16], BF16, esAs), "qkp": sgh[:NS, :],
                           "qk": P.sbuf("qk", [16, 4], F32, esAs)})
                SB["osb"] = SB["s0b"][0].all().re("p h v -> p (h v)")[:NS, :]
                pos = core_sample()
                merge(tail(16, pos))
                P.barrier()

        if stop_after == "A":
            r_ = _finish(nc, P, D, resid, debug_resid)
            es_hT.close()
            return r_

        NB = 256
        with ExitStack() as esB:
            WscB = [P.sbuf("Wsc%d" % c, [128, 8, 4, 128], BF16, esB) for c in range(8)]
            Wob = P.sbuf("Wob", [128, 8, 1024], BF16, esB)
            for c in range(8):
                P.dma("sp", WscB[c].all().re("p kc g n -> p (kc g n)"), WSCD[c])
            P.dma("sp", Wob.all().re("p kc n -> p (kc n)"), WOBD.all())
            dg3 = P.sbuf("dg3", [128, 8, 3, 128], BF16, esB)
            for c in range(8):
                for j in range(3):
                    P.ts("dve", dg3[:, c, j, :], identb.all(), smalls["wsT"][:, c, j:j + 1], ALU.mult)
            hbs = [P.sbuf("hbs%d" % i, [128, NB], F32, esB) for i in range(2)]
            szb = [P.sbuf("szb%d" % i, [128, NB], F32, esB) for i in range(2)]
            yT = P.sbuf("yT", [128, 8, NB], BF16, esB)

            def phaseB_tile(T, sample, B, hook=None):
                N = NS if sample else NB
                col0 = TOK if sample else T * NB
                last = (not sample) and (T == TOK // NB - 1)
                if sample:
                    sscb, ue, usf, usT = B["sscb"], B["ue"], B["usf"], B["usT"]
                    P.dma("sp", D["scs"][:, 0, :], D["ssc"][:, 1, :])
                    pt = ring.get()
                    for j in range(2):
                        P.dma("sp", sscb.all(), D["ssc"][:, j, :])
                        for c in range(8):
                            jc = j * 8 + c
                            P.tr(pt[:, jc * 16:(jc + 1) * 16], sscb[:, c * 128:(c + 1) * 128], identf[:16, :16],
                                 inc=True)
                    for j in range(2):
                        P.copy("act", ue[:, :, j, :], pt[:, j * 128:(j + 1) * 128].re("p (c s) -> p c s", c=8))
                else:
                    ub, ust, ustT = B["ub"], B["ust"], B["ustT"]
                for c in range(8):
                    ps = []
                    for g in range(4):
                        pb_ = ring.get()
                        for kc in range(8):
                            P.mm(pb_[:, :N], WscB[c][:, kc, g, :], hT[:, kc, col0:col0 + N],
                                 start=(kc == 0), stop=(kc == 7))
                        ps.append(pb_)
                    phb, pgb, pgc, pzb = ps
                    hb = hbs[c % 2]
                    sz = szb[c % 2]
                    P.copy("act", hb[:, :N], phb[:, :N])
                    py = ring.get()
                    if not sample:
                        if T > 0:
                            P.copy("dve", ub[c][:, 0:2], ub[c][:, NB:NB + 2])
                        P.tt("dve", ub[c][:, 2:NB + 2], pgc[:, :N], hb[:, :N], ALU.mult)
                        if last:
                            P.tt("dve", ust[:, :, c], pgc[:, NB - 2:NB], hb[:, NB - 2:NB], ALU.mult)
                        for j in range(3):
                            P.mm(py[:, :N], dg3[:, c, j, :], ub[c][:, j:j + NB], start=(j == 0), stop=(j == 2))
                    else:
                        P.tt("dve", usf[:, c, :], pgc[:, :N], hb[:, :N], ALU.mult)
                        P.copy("dve", ue[:, c, 2, :], usf[:, c, :])
                        for j in range(3):
                            P.mm(py[:, :N], dg3[:, c, j, :], ue[:, c, j, :], start=(j == 0), stop=(j == 2))
                    P.act(sz[:, :N], pzb[:, :N], AF.Silu)
                    P.tt("dve", sz[:, :N], pgb[:, :N], sz[:, :N], ALU.mult)
                    P.tt("dve", yT[:, c, :N], py[:, :N], sz[:, :N], ALU.mult)
                    if hook is not None:
                        hook(c)
                nsub = 1 if sample else NB // 128
                for s in range(nsub):
                    ntok = NS if sample else 128
                    ti = 16 if sample else T * nsub + s
                    for n in range(2):
                        pout = ring.get()
                        for c in range(8):
                            P.mm(pout[:ntok, :], yT[:, c, s * 128:s * 128 + ntok], Wob[:, c, n * 512:(n + 1) * 512],
                                 start=(c == 0), stop=(c == 7))
                        P.tt("dve", resid[ti][:ntok, n * 512:(n + 1) * 512], pout[:ntok, :],
                             resid[ti][:ntok, n * 512:(n + 1) * 512], ALU.add)
                    for n in range(2):
                        P.act(ring.get()[:ntok, :], resid[ti][:ntok, n * 512:(n + 1) * 512], AF.Square,
                              accum_out=ssq1h[:ntok, ti, n:n + 1])
                if last:
                    pt = ring.get()
                    P.tr(pt[:16, 0:128], ust.all().re("p j c -> p (j c)"), identf.all())
                    P.copy("act", ustT.all(), pt[:16, 0:128])
                    P.dma("sp", D["scp"].all().re("j (c p) -> (j c) p", p=128), ustT.all())
                if sample:
                    for half in range(2):
                        pt = ring.get()
                        for cc in range(4):
                            c = half * 4 + cc
                            P.tr(pt[:16, cc * 128:(cc + 1) * 128], usf[:, c, :], identf.all(), inc=(cc == 3))
                        P.copy("act", usT[:, half * 512:(half + 1) * 512], pt[:16, :])
                    P.dma("sp", D["scs"][:, 1, :], usT.all())

            with ExitStack() as esBs:
                B = {"sscb": P.sbuf("sscb", [16, 1024], F32, esBs), "ue": P.sbuf("ue", [128, 8, 3, 16], BF16, esBs),
                     "usf": P.sbuf("usf", [128, 8, 16], F32, esBs)}
                B["usT"] = B["sscb"]
                phaseB_tile(0, True, B)
                P.barrier()
            with ExitStack() as esBp:
                B = {"ub": [P.sbuf("ub%d" % c, [128, NB + 2], BF16, esBp) for c in range(8)],
                     "ust": P.sbuf("ust", [128, 2, 8], F32, esBp), "ustT": P.sbuf("ustT", [16, 128], F32, esBp)}
                for c in range(8):
                    P.memset("dve", B["ub"][c][:, 0:2], 0.0)
                dgb = P.sbuf("dgb", [128, 16, 128], BF16, esBp)
                def dg_hook(T):
                    def hook(c):
                        cc = T
                        dsrc = D["dgd"][cc].re("p (j q) -> p j q", j=31)
                        j0, j1 = c * 4, min(31, c * 4 + 4)
                        base = 0 if c < 4 else 16
                        for j in range(j0, j1):
                            P.ts("dve", dgb[:, j - base, :], identb.all(), smalls["wdT"][:, cc, j:j + 1], ALU.mult)
                        if c == 3:
                            P.dma("sp", dsrc[:, 0:16, :], dgb[:, 0:16, :])
                        if c == 7:
                            P.dma("sp", dsrc[:, 16:31, :], dgb[:, 0:15, :])
                    return hook

                for T in range(TOK // NB):
                    phaseB_tile(T, False, B, hook=dg_hook(T))
                P.barrier()

        if stop_after == "B":
            r_ = _finish(nc, P, D, resid, debug_resid)
            es_hT.close()
            return r_
        es_hT.close()

        with ExitStack() as esC:
            WicB = [P.sbuf("Wic%d" % c, [128, 8, 3, 128], BF16, esC) for c in range(8)]
            Woc = P.sbuf("Woc", [128, 8, 1024], BF16, esC)
            brows = P.sbuf("brows", [64, 1024], BF16, esC)
            boutb = brows
            P.dma("pool", brows[0:1, :], D["bout"].all())
            bh = P.sbuf("bh", [128, 24], F32, esC)
            P.ts("dve", bh.all(), smalls["binc"].all(), 0.5, ALU.mult)
            P.dma("pool", brows[32:33, :], D["bincrow"][:, 0:1024])
            P.ts("dve", brows[32:33, :], brows[32:33, :], 0.5, ALU.mult)
            for c in range(8):
                P.dma("sp", WicB[c].all().re("p kc g n -> p (kc g n)"), WICD[c])
                P.ts("dve", WicB[c][:, :, 0, :], WicB[c][:, :, 0, :], 0.5, ALU.mult)
            P.dma("sp", Woc.all().re("p kc n -> p (kc n)"), WOCD.all())
            fgbc = P.sbuf("fgbc", [128, 1024], F32, esC)
            P.dma("sp", fgbc.all(), D["fgbc"].all())
            NT = 256
            dgr = [P.sbuf("dgr%d" % i, [128, 16, 128], BF16, esC) for i in range(4)]
            dg_built = [True]
            P.tt("dve", ssq1.all(), ssq1h[:, :, 0], ssq1h[:, :, 1], ALU.add)
            rstd_from_ssq(rstd1.all(), ssq1.all(), 1024, RMS_EPS)
            ufl = P.sbuf("ufl", [128, 8, 30], F32, esC)
            ssq2 = P.sbuf("ssq2", [128, 1], F32, esC)
            ssq2h = P.sbuf("ssq2h", [128, 2], F32, esC)
            rstd2 = P.sbuf("rstd2", [128, 1], F32, esC)
            xnC = P.sbuf("xnC", [128, 1024], BF16, esC)
            dgi = [0]

            FRC = {}

            def cinfo(Tc, sample):
                N = NS if sample else NT
                nsub = 1 if sample else NT // 128
                ntok = NS if sample else 128
                last = (not sample) and (Tc == TOK // NT - 1)
                return N, nsub, ntok, last

            def frontC(Tc, sample, C):
                N, nsub, ntok, last = cinfo(Tc, sample)
                b = Tc % 2
                h1T, yc = C["h1T"][b], C["yc"][b]
                sgb, yst = C["sgb"], C["yst"]
                for s in range(nsub):
                    ti = 16 if sample else Tc * nsub + s
                    P.ts("dve", xnC[:ntok, :], resid[ti][:ntok, :], rstd1[:ntok, ti:ti + 1], ALU.mult)
                    transposes_to(lambda: h1T[:, :, s * 128:s * 128 + ntok], xnC, ntok, 8, identb, e="dve",
                                  gT=smalls["g1T"])
                    yield 2.5
                if sample:
                    sccb, bT, ucs, ufs = C["sccb"], C["bT"], C["ucs"], C["ufs"]
                    P.dma("sp", D["ccs"][:, 0:29, :], D["scc"][:, 1:30, :])
                    for q in range(4):
                        P.dma("sp", sccb.all(), D["scc"].all().re("s j c -> (s j) c")[q * 120:(q + 1) * 120, :])
                        for c in range(8):
                            pt = ring.get()
                            P.tr(pt[:, 0:120], sccb[:, c * 128:(c + 1) * 128], identf[:120, :120])
                            P.copy("act", bT[:, c, q * 120:(q + 1) * 120], pt[:, 0:120])
                else:
                    uc = C["uc"]
                pst = ring.get(hold=True)

                def proj_c(c):
                    pab = ring.get(hold=True)
                    for n0 in range(0, N, 128):
                        n1 = min(N, n0 + 128)
                        P.mm(pab[:, n0:n1], brows[32:33, c * 128:(c + 1) * 128], onesb[32:33, 0:n1 - n0],
                             start=(n0 == 0), stop=False)
                    for kc in range(8):
                        P.mm(pab[:, 0:N], WicB[c][:, kc, 0, :], h1T[:, kc, :N], start=False, stop=(kc == 7))
                    for kc in range(8):
                        P.mm(pab[:, 256:256 + N], WicB[c][:, kc, 1, :], h1T[:, kc, :N],
                             start=(kc == 0), stop=(kc == 7))
                    return pab

                nxt = proj_c(0)
                pend = []
                for c in range(8):
                    pab = nxt
                    pa, pag = pab[:, 0:N], pab[:, 256:256 + N]
                    sgc = sgb[c % 2]
                    P.act(sgc[:, :N], pag, AF.Tanh, scale=0.5, bias=bh[:, 8 + c:9 + c])
                    dgh = [dgr[(dgi[0] + k) % 4] for k in range(2)]
                    dgi[0] += 2
                    dsrc = D["dgd"][c].re("p (j q) -> p j q", j=31)
                    for k, (j0, j1) in enumerate(((0, 16), (16, 31))):
                        if not dg_built[0]:
                            for j in range(j0, j1):
                                P.ts("dve", dgh[k][:, j - j0, :], identb.all(), smalls["wdT"][:, c, j:j + 1], ALU.mult)
                            P.dma("act", dsrc[:, j0:j1, :], dgh[k][:, 0:j1 - j0, :])
                        else:
                            P.dma("sp", dgh[k][:, 0:j1 - j0, :], dsrc[:, j0:j1, :])

                    def dgt_tap(j):
                        return dgh[j // 16][:, j % 16, :]
                    if not sample:
                        P.stt(uc[c][:, 30:30 + NT], sgc[:, :N], 1.0, pa, ALU.add, ALU.mult)
                        if last:
                            P.stt(ufl[:, c, :], sgc[:, NT - 30:NT], 1.0, pab[:, NT - 30:NT], ALU.add, ALU.mult)
                    else:
                        P.stt(ufs[:, c, :], sgc[:, :N], 1.0, pa, ALU.add, ALU.mult)
                        P.copy("dve", ucs[:, c, :], ufs[:, c, :])
                    ring.release(pab)
                    if c + 1 < 8:
                        nxt = proj_c(c + 1)
                    yield 1.7
                    py = ring.get()
                    if not sample:
                        for j in range(31):
                            P.mm(py[:, :N], dgt_tap(j), uc[c][:, j:j + NT], start=(j == 0), stop=(j == 30))
                        if not last:
                            P.copy("pool", uc[c][:, 0:30], uc[c][:, NT:NT + 30])
                    else:
                        for j in range(30):
                            P.mm(py[:, :N], dgt_tap(j), bT[:, c, :].re("p (s j) -> p j s", j=30)[:, j, :],
                                 start=(j == 0), stop=False)
                        P.mm(py[:, :N], dgt_tap(30), ucs[:, c, :], start=False, stop=True)
                    yield 3.4
                    while pend:
                        pend.pop(0)()
                    ys = yst[c % 2]
                    P.act(yc[:, c, :N], py[:, :N], AF.Identity, bias=smalls["bdw"][:, c:c + 1])
                    P.act(ys[:, 1, :], py[:, :N], AF.Square, bias=smalls["bdw"][:, c:c + 1])
                    P.copy("pool", ys[:, 0, :], yc[:, c, :N])
                    pend.append(lambda ys=ys, c=c: P.mm(pst[:, 0:2 * N], onesdiv.all(), ys.all().re("p a n -> p (a n)"),
                                                        start=(c == 0), stop=(c == 7)))
                    yield 1.3
                while pend:
                    pend.pop(0)()
                dg_built[0] = True
                FRC[(Tc, sample)] = pst

            def backC(Tc, sample, C):
                N, nsub, ntok, last = cinfo(Tc, sample)
                b = Tc % 2
                h1T, yc = C["h1T"][b], C["yc"][b]
                msq, var, dd, t2, sl, szc, ycT = C["msq"], C["var"], C["dd"], C["t2"], C["sl"], C["szc"], C["ycT"]
                pst = FRC.pop((Tc, sample))
                mean, ex2 = pst[:, 0:N], pst[:, N:2 * N]
                P.act(msq[:, :N], mean, AF.Square)
                P.tt("dve", var[:, :N], ex2, msq[:, :N], ALU.subtract)
                P.act(var[:, :N], var[:, :N], AF.Ln, bias=eps_t[LN_EPS][:, 0:1])
                P.act(var[:, :N], var[:, :N], AF.Exp, scale=-0.5)
                prn = ring.get(hold=True)
                prs, nmr = prn[:, 0:N], prn[:, 256:256 + N]
                P.copy("act", prs, var[:, :N])
                P.stt(nmr, mean, -1.0, var[:, :N], ALU.mult, ALU.mult)
                ring.release(pst)
                yield 3.0
                for c in range(8):
                    P.tt("dve", dd[c % 2][:, :N], yc[:, c, :N], prs, ALU.mult)
                    P.tt("dve", t2[c % 2][:, :N], dd[c % 2][:, :N], nmr, ALU.add)
                    pz = ring.get()
                    for kc in range(8):
                        P.mm(pz[:, :N], WicB[c][:, kc, 2, :], h1T[:, kc, :N],
                             start=(kc == 0), stop=(kc == 7))
                    P.act(sl[c % 2][:, :N], t2[c % 2][:, :N], AF.Silu, scale=smalls["lng"][:, c:c + 1],
                          bias=smalls["lnb"][:, c:c + 1])
                    P.act(szc[c % 2][:, :N], pz[:, :N], AF.Silu, bias=smalls["binc"][:, 16 + c:17 + c])
                    P.tt("pool", ycT[:, c, :N], sl[c % 2][:, :N], szc[c % 2][:, :N], ALU.mult)
                    yield 2.6
                ring.release(prn)
                for s in range(nsub):
                    ti = 16 if sample else Tc * nsub + s
                    for n in range(2):
                        pout = ring.get()
                        P.mm(pout[:ntok, :], onesb[0:1, :ntok], boutb[0:1, n * 512:(n + 1) * 512], start=True, stop=False)
                        for c in range(8):
                            P.mm(pout[:ntok, :], ycT[:, c, s * 128:s * 128 + ntok], Woc[:, c, n * 512:(n + 1) * 512],
                                 start=False, stop=(c == 7))
                        P.tt("dve", resid[ti][:ntok, n * 512:(n + 1) * 512], pout[:ntok, :],
                             resid[ti][:ntok, n * 512:(n + 1) * 512], ALU.add)
                        yield 2.6
                    for n in range(2):
                        P.act(ring.get()[:ntok, :], resid[ti][:ntok, n * 512:(n + 1) * 512], AF.Square,
                              accum_out=ssq2h[:ntok, n:n + 1])
                    P.tt("dve", ssq2[:ntok, :], ssq2h[:ntok, 0:1], ssq2h[:ntok, 1:2], ALU.add)
                    rstd_from_ssq(rstd2[:ntok, :], ssq2[:ntok, :], 1024, RMS_EPS)
                    P.stt(resid[ti][:ntok, :], resid[ti][:ntok, :], rstd2[:ntok, :], fgbc[:ntok, :], ALU.mult, ALU.mult)
                    if sample:
                        P.dma("sp", D["ys"].all(), resid[ti][:ntok, :])
                    else:
                        P.dma("sp", D["y"][ti * 128:(ti + 1) * 128, :], resid[ti].all())
                    yield 4.0
                if sample:
                    ufs, usTc = C["ufs"], C["usTc"]
                    for half in range(2):
                        pt = ring.get()
                        for cc in range(4):
                            c = half * 4 + cc
                            P.tr(pt[:16, cc * 128:(cc + 1) * 128], ufs[:, c, :], identf.all(), inc=(cc == 3))
                        P.copy("act", usTc[:, half * 512:(half + 1) * 512], pt[:16, :])
                    P.dma("sp", D["ccs"][:, 29, :], usTc.all())

            def work_bufs(esx, N, nb):
                return {"h1T": [P.sbuf("h1T%d" % i, [128, 8, N], BF16, esx) for i in range(nb)],
                        "yc": [P.sbuf("yc%d" % i, [128, 8, N], F32, esx) for i in range(nb)],
                        "sgb": [P.sbuf("sgb%d" % i, [128, N], F32, esx) for i in range(2)],
                        "yst": [P.sbuf("yst%d" % i, [128, 2, N], BF16, esx) for i in range(2)],
                        "msq": P.sbuf("msq", [128, N], F32, esx), "var": P.sbuf("var", [128, N], F32, esx),
                        "dd": [P.sbuf("dd%d" % i, [128, N], F32, esx) for i in range(2)],
                        "t2": [P.sbuf("t2%d" % i, [128, N], F32, esx) for i in range(2)],
                        "sl": [P.sbuf("sl%d" % i, [128, N], F32, esx) for i in range(2)],
                        "szc": [P.sbuf("szc%d" % i, [128, N], F32, esx) for i in range(2)],
                        "ycT": P.sbuf("ycT", [128, 8, N], BF16, esx)}

            MERGE_W = (1, 1)

            def mergeC(*gens):
                gens = [g for g in gens if g is not None]
                w = list(MERGE_W[:len(gens)]) if len(gens) > 1 else [1]
                while gens:
                    for gi, g in enumerate(list(gens)):
                        for _ in range(w[gi] if gi < len(w) else 1):
                            try:
                                next(g)
                            except StopIteration:
                                if g in gens:
                                    gens.remove(g)
                                break

            with ExitStack() as esCs:
                C = work_bufs(esCs, NS, 1)
                C.update({"sccb": P.sbuf("sccb", [120, 1024], F32, esCs), "bT": P.sbuf("bT", [128, 8, 480], BF16, esCs),
                          "ucs": P.sbuf("ucs", [128, 8, 16], BF16, esCs), "ufs": P.sbuf("ufs", [128, 8, 16], F32, esCs),
                          "usTc": P.sbuf("usTc", [16, 1024], F32, esCs)})
                mergeC(frontC(0, True, C))
                mergeC(backC(0, True, C))
                P.barrier()
            with ExitStack() as esCp:
                C = work_bufs(esCp, NT, 2)
                C["uc"] = [P.sbuf("uc%d" % c, [128, 30 + NT], BF16, esCp) for c in range(8)]
                for c in range(8):
                    P.memset("dve", C["uc"][c][:, 0:30], 0.0)
                nT = TOK // NT
                mergeC(frontC(0, False, C))
                for Tc in range(nT):
                    mergeC(frontC(Tc + 1, False, C) if Tc + 1 < nT else None, backC(Tc, False, C))
                P.barrier()
            with ExitStack() as esCe:
                ccpb = P.sbuf("ccpb", [30, 1024], F32, esCe)
                for half in range(2):
                    pt = ring.get()
                    for cc in range(4):
                        c = half * 4 + cc
                        P.tr(pt[:30, cc * 128:(cc + 1) * 128], ufl[:, c, :], identf.all(), inc=(cc == 3))
                    P.copy("act", ccpb[:, half * 512:(half + 1) * 512], pt[:30, :])
                P.dma("sp", D["ccp"].all(), ccpb.all())
                P.barrier()
        return _finish(nc, P, D, resid, debug_resid)


def _finish(nc, P, D, resid, debug_resid):
    if debug_resid:
        for i in range(17):
            P.dma("sp", D["dbg"][i], resid[i].all())
    P.barrier()
    print("ninstr", P.ninstr, "nsem", len(P.semobj))
    return nc


def _consts():
    identf = np.eye(128, dtype=np.float32)
    identb = identf.astype(ml_dtypes.bfloat16)
    trif = np.triu(np.ones((128, 128), dtype=np.float32))
    delta = np.broadcast_to(np.eye(16, dtype=np.float32)[None], (128, 16, 16)).copy()
    return {"c_identb": identb, "c_identf": identf, "c_trif": trif, "c_delta": delta}


def _fm(v, n):
    return np.ascontiguousarray(np.asarray(v, dtype=np.float32).reshape(n, 128).T)


def make_in_maps(inp):
    f = lambda k: np.asarray(inp[k], dtype=np.float32)
    shared = {
        "w_in_a": np.ascontiguousarray(f("w_in_a")[0]),
        "w_out_a": np.ascontiguousarray(f("w_out_a")[0]),
        "w_in_c": np.ascontiguousarray(f("w_in_c")[0]),
        "w_out_c": np.ascontiguousarray(f("w_out_c")[0]),
        "wgu": np.ascontiguousarray(np.concatenate([f("w_gate_up")[0], f("b_gate_up")[0][None, :]], axis=0)),
        "g0T": _fm(f("norm_g")[0], 8),
        "g1T": _fm(f("norm_g")[1], 8),
        "glaT": _fm(np.tile(f("gla_norm_g")[0], 4), 8),
        "wsT": np.ascontiguousarray(f("w_sconv")[0].T.reshape(8, 128, 3).transpose(1, 0, 2)),
        "wdT": np.ascontiguousarray(f("w_dwconv")[0].T.reshape(8, 128, 31).transpose(1, 0, 2)),
        "bdw": _fm(f("b_dwconv")[0], 8),
        "lng": _fm(f("ln_g")[0], 8),
        "lnb": _fm(f("ln_b")[0], 8),
        "binc": _fm(f("b_in_c")[0], 24),
        "bout": np.ascontiguousarray(f("b_out_c")[0][None, :]),
        "bincrow": np.ascontiguousarray(f("b_in_c")[0][None, :]),
        "fgbc": np.ascontiguousarray(np.broadcast_to(f("final_norm_g")[None, :], (128, 1024))),
    }
    shared.update(_consts())
    xp, xs = f("x_prompt"), f("x_sample")
    sg, ss, sc = f("state_gla"), f("state_sconv"), f("state_cconv")
    maps = []
    for b in range(NCORES):
        m = dict(shared)
        sl = slice(b * NS, (b + 1) * NS)
        m["x"] = np.ascontiguousarray(xp[b])
        m["xs"] = np.ascontiguousarray(xs[sl, 0, :])
        m["sgla"] = np.ascontiguousarray(sg[0, sl])
        m["ssc"] = np.ascontiguousarray(ss[0, sl])
        m["scc"] = np.ascontiguousarray(sc[0, sl])
        maps.append(m)
    return maps


_NC_CACHE = {}


def kernel(**inputs):
    if "nc" not in _NC_CACHE:
        _NC_CACHE["nc"] = build_nc()
    nc = _NC_CACHE["nc"]
    maps = make_in_maps(inputs)
    res = run_bass_kernel_spmd(nc, maps, core_ids=list(range(NCORES)))
    R = res.results
    y_prompt = np.stack([R[b]["y"] for b in range(NCORES)], axis=0)
    y_sample = np.concatenate([R[b]["ys"] for b in range(NCORES)], axis=0)[:, None, :]
    gla_p = np.stack([R[b]["glap"] for b in range(NCORES)], axis=0)[None]
    sconv_p = np.stack([R[b]["scp"] for b in range(NCORES)], axis=0)[None]
    cconv_p = np.stack([R[b]["ccp"] for b in range(NCORES)], axis=0)[None]
    gla_s = np.concatenate([R[b]["glas"] for b in range(NCORES)], axis=0)[None]
    sconv_s = np.concatenate([R[b]["scs"] for b in range(NCORES)], axis=0)[None]
    cconv_s = np.concatenate([R[b]["ccs"] for b in range(NCORES)], axis=0)[None]
    outs = (y_prompt, y_sample, gla_p, sconv_p, cconv_p, gla_s, sconv_s, cconv_s)
    return tuple(np.ascontiguousarray(o, dtype=np.float32) for o in outs)
```

```python
import numpy as np
import ml_dtypes
from contextlib import ExitStack
import concourse.bass as bass
import concourse.mybir as mybir
from concourse.bass_utils import run_bass_kernel_spmd

F32 = mybir.dt.float32
BF16 = mybir.dt.bfloat16
AF = mybir.ActivationFunctionType
ALU = mybir.AluOpType

SAME_ENGINE_SYNC = True
RMS_EPS = 1e-6
LN_EPS = 1e-5
NCORES = 8
NS = 16
TOK = 2048
STOP_AFTER = None


class Buf:
    def __init__(self, name, t):
        self.name = name
        self.t = t
        self.writes = {}
        self.reads = {}
        self.dsem = None
        self.dcnt = 0

    def __getitem__(self, idx):
        return View(self, self.t[idx])

    def all(self):
        return View(self, self.t[:])


class View:
    def __init__(self, buf, ap):
        self.buf = buf
        self.ap = ap

    def __getitem__(self, idx):
        return View(self.buf, self.ap[idx])

    def re(self, pat, **kw):
        return View(self.buf, self.ap.rearrange(pat, **kw))

    def bc(self, shape):
        return View(self.buf, self.ap.broadcast_to(list(shape)))

    def cast(self, dt):
        return View(self.buf, self.ap.bitcast(dt))


def _ap(v):
    return v.ap if isinstance(v, View) else v


class Ring:
    def __init__(self, bufs):
        self.bufs = bufs
        self.i = 0
        self.held = set()

    def get(self, hold=False):
        for _ in range(len(self.bufs) + 1):
            b = self.bufs[self.i]
            self.i = (self.i + 1) % len(self.bufs)
            if id(b) not in self.held:
                if hold:
                    self.held.add(id(b))
                return b
        raise RuntimeError("ring exhausted")

    def release(self, b):
        self.held.discard(id(b))


class Prog:
    ENG = ("pe", "dve", "act", "pool", "sp")

    def __init__(self, nc, es):
        self.nc = nc
        self.es = es
        self.eng = {"pe": nc.tensor, "dve": nc.vector, "act": nc.scalar, "pool": nc.gpsimd, "sp": nc.sync}
        self.semobj = {}
        self.cnt = {}
        for k in self.ENG:
            self.semobj[k] = es.enter_context(nc.semaphore("s_" + k))
            self.cnt[k] = 0
        self.dcnts = {}
        self.known = {k: {} for k in self.ENG}
        self.nbuf = 0
        self.ninstr = {k: 0 for k in self.ENG}

    def sbuf(self, name, shape, dt, es=None):
        self.nsb = getattr(self, "nsb", 0) + 1
        t = (es or self.es).enter_context(self.nc.sbuf_tensor("sb%d_%s" % (self.nsb, name), list(shape), dt))
        return Buf(name, t)

    def psum(self, name, shape, dt):
        t = self.es.enter_context(self.nc.psum_tensor(name, list(shape), dt))
        return Buf(name, t)

    def dram(self, name, ap):
        b = Buf(name, ap)
        b.is_dram = True
        return b

    def _emit_waits(self, e, deps):
        for k, v in deps.items():
            if self.known[e].get(k, 0) >= v:
                continue
            self.eng[e].wait_ge(self.semobj[k], v)
            self.ninstr[e] += 1
            self.known[e][k] = v

    def _deps(self, e, reads, writes):
        deps = {}

        def merge(d, skip_self):
            for k, v in d.items():
                if k == e and skip_self:
                    continue
                if deps.get(k, 0) < v:
                    deps[k] = v

        for r in reads:
            merge(r.buf.writes, not SAME_ENGINE_SYNC)
        skip_w = (e == "pe") or (not SAME_ENGINE_SYNC)
        for w in writes:
            merge(w.buf.writes, skip_w)
            merge(w.buf.reads, skip_w)
        return deps

    def _record(self, key, val, reads, writes):
        for r in reads:
            if r.buf.reads.get(key, 0) < val:
                r.buf.reads[key] = val
        for w in writes:
            if w.buf.reads:
                w.buf.reads = {}
                w.buf.writes = {}
            w.buf.writes[key] = val

    def op(self, e, fn, reads=(), writes=(), inc=True):
        reads = [r for r in reads if isinstance(r, View)]
        writes = [w for w in writes if isinstance(w, View)]
        self._emit_waits(e, self._deps(e, reads, writes))
        ins = fn()
        self.ninstr[e] += 1
        if inc:
            self.cnt[e] += 1
            ins.then_inc(self.semobj[e], 1)
            val = self.cnt[e]
        else:
            val = self.cnt[e] + 1
        self._record(e, val, reads, writes)
        return ins

    def dma(self, q, out, in_, **kw):
        reads = [in_]
        writes = [out]
        self._emit_waits(q, self._deps(q, reads, writes))
        owner = out.buf
        if getattr(out.buf, "is_dram", False) and not getattr(in_.buf, "is_dram", False):
            owner = in_.buf
        kind = "sw" if q == "pool" else "hw"
        if not hasattr(owner, "dsems"):
            owner.dsems = {}
            owner.dcnts_ = {}
        if kind not in owner.dsems:
            nm = "d%d%s_%s" % (self.nbuf, kind[0], owner.name)
            self.nbuf += 1
            owner.dsems[kind] = nm
            owner.dcnts_[kind] = 0
            self.semobj[nm] = self.es.enter_context(self.nc.semaphore(nm))
        nm = owner.dsems[kind]
        ins = self.eng[q].dma_start(out=out.ap, in_=in_.ap, **kw)
        self.ninstr[q] += 1
        owner.dcnts_[kind] += 16
        self.dcnts[nm] = owner.dcnts_[kind]
        ins.then_inc(self.semobj[nm], 16)
        self._record(nm, owner.dcnts_[kind], reads, writes)
        return ins

    def barrier(self):
        deps = {}
        for k in self.ENG:
            if self.cnt[k] > 0:
                deps[k] = self.cnt[k]
        deps.update(self.dcnts)
        for e in self.ENG:
            self._emit_waits(e, dict(deps))

    def mm(self, out, lhsT, rhs, start=True, stop=True, inc=None):
        if inc is None:
            inc = stop
        return self.op("pe", lambda: self.nc.tensor.matmul(_ap(out), _ap(lhsT), _ap(rhs), start=start, stop=stop),
                       reads=[lhsT, rhs], writes=[out], inc=inc)

    def tr(self, out, in_, ident, inc=True):
        return self.op("pe", lambda: self.nc.tensor.transpose(_ap(out), _ap(in_), _ap(ident)),
                       reads=[in_, ident], writes=[out], inc=inc)

    def act(self, out, in_, func, scale=1.0, bias=None, accum_out=None):
        reads = [in_, scale, bias]
        writes = [out, accum_out]
        kw = {}
        if bias is not None:
            kw["bias"] = _ap(bias)
        if accum_out is not None:
            kw["accum_out"] = _ap(accum_out)
        return self.op("act", lambda: self.nc.scalar.activation(out=_ap(out), in_=_ap(in_), func=func,
                                                                scale=_ap(scale), **kw),
                       reads=reads, writes=writes)

    def ts(self, e, out, in0, s1, op0, s2=None, op1=None):
        kw = {}
        if op1 is not None:
            kw["op1"] = op1
        return self.op(e, lambda: self.eng[e].tensor_scalar(out=_ap(out), in0=_ap(in0), scalar1=_ap(s1),
                                                            scalar2=_ap(s2), op0=op0, **kw),
                       reads=[in0, s1, s2], writes=[out])

    def tt(self, e, out, in0, in1, op):
        return self.op(e, lambda: self.eng[e].tensor_tensor(out=_ap(out), in0=_ap(in0), in1=_ap(in1), op=op),
                       reads=[in0, in1], writes=[out])

    def stt(self, out, in0, scalar, in1, op0, op1):
        return self.op("dve", lambda: self.nc.vector.scalar_tensor_tensor(out=_ap(out), in0=_ap(in0),
                                                                        scalar=_ap(scalar), in1=_ap(in1),
                                                                        op0=op0, op1=op1),
                       reads=[in0, in1, scalar], writes=[out])

    def copy(self, e, out, in_):
        if e == "act":
            return self.op("act", lambda: self.nc.scalar.copy(out=_ap(out), in_=_ap(in_)), reads=[in_], writes=[out])
        return self.op(e, lambda: self.eng[e].tensor_copy(out=_ap(out), in_=_ap(in_)), reads=[in_], writes=[out])

    def memset(self, e, out, val):
        return self.op(e, lambda: self.eng[e].memset(_ap(out), val), reads=[], writes=[out])


IN_SPECS = [
    ("x", [TOK, 1024], F32), ("xs", [NS, 1024], F32), ("sgla", [NS, 4, 128, 256], F32),
    ("ssc", [NS, 2, 1024], F32), ("scc", [NS, 30, 1024], F32),
    ("w_in_a", [1024, 7184], F32), ("w_out_a", [2048, 1024], F32),
    ("w_in_c", [1024, 3072], F32), ("w_out_c", [1024, 1024], F32),
    ("wgu", [17, 512], F32), ("g0T", [128, 8], F32), ("g1T", [128, 8], F32), ("glaT", [128, 8], F32),
    ("wsT", [128, 8, 3], F32), ("wdT", [128, 8, 31], F32), ("bdw", [128, 8], F32),
    ("lng", [128, 8], F32), ("lnb", [128, 8], F32), ("binc", [128, 24], F32),
    ("bout", [1, 1024], F32), ("bincrow", [1, 3072], F32), ("fgbc", [128, 1024], F32),
    ("c_identb", [128, 128], BF16), ("c_identf", [128, 128], F32), ("c_trif", [128, 128], F32),
    ("c_delta", [128, 16, 16], F32),
]
OUT_SPECS = [
    ("y", [TOK, 1024], F32), ("ys", [NS, 1024], F32), ("glap", [4, 128, 256], F32),
    ("scp", [2, 1024], F32), ("ccp", [30, 1024], F32), ("glas", [NS, 4, 128, 256], F32),
    ("scs", [NS, 2, 1024], F32), ("ccs", [NS, 30, 1024], F32),
]


def build_nc(stop_after=None, debug_resid=False):
    nc = bass.Bass("TRN2", target_bir_lowering=False)
    D = {}
    with ExitStack() as es:
        P = Prog(nc, es)
        for name, shape, dt in IN_SPECS:
            D[name] = P.dram(name, nc.dram_tensor(name, shape, dt, kind="ExternalInput").ap())
        for name, shape, dt in OUT_SPECS:
            D[name] = P.dram(name, nc.dram_tensor(name, shape, dt, kind="ExternalOutput").ap())
        if debug_resid:
            D["dbg"] = P.dram("dbg", nc.dram_tensor("dbg", [17, 128, 1024], F32, kind="ExternalOutput").ap())
        D["dgd"] = P.dram("dgd", nc.dram_tensor("dgd", [8, 128, 31 * 128], BF16, kind="Internal").ap())
        wscd_t = nc.dram_tensor("wscd", [8, 128, 4096], BF16, kind="Internal").ap()
        wicd_t = nc.dram_tensor("wicd", [8, 128, 3072], BF16, kind="Internal").ap()
        wscd_b = P.dram("wscd", wscd_t)
        wicd_b = P.dram("wicd", wicd_t)
        WSCD = [wscd_b[c] for c in range(8)]
        WICD = [wicd_b[c] for c in range(8)]
        WOBD = P.dram("wobd", nc.dram_tensor("wobd", [128, 8192], BF16, kind="Internal").ap())
        WOCD = P.dram("wocd", nc.dram_tensor("wocd", [128, 8192], BF16, kind="Internal").ap())

        def precast_B():
            wsrc = D["w_in_a"][:, 3088:7184].re("(kc p) (g c n) -> p kc g c n", p=128, g=4, c=8)
            for c in range(8):
                dst = WSCD[c].re("p (kc g n) -> p kc g n", kc=8, g=4)
                for g in range(4):
                    P.dma("pool", dst[:, :, g, :], wsrc[:, :, g, c, :])
            dst = WOBD.all().re("p (kc n) -> p kc n", kc=8)
            for h in range(2):
                P.dma("pool", dst[:, :, h * 512:(h + 1) * 512],
                      D["w_out_a"][1024:2048, h * 512:(h + 1) * 512].re("(kc p) n -> p kc n", p=128))

        def precast_C():
            wsrc = D["w_in_c"].all().re("(kc p) (g c n) -> p kc g c n", p=128, g=3, c=8)
            for c in range(8):
                dst = WICD[c].re("p (kc g n) -> p kc g n", kc=8, g=3)
                for g in range(3):
                    P.dma("pool", dst[:, :, g, :], wsrc[:, :, g, c, :])
            dst = WOCD.all().re("p (kc n) -> p kc n", kc=8)
            for h in range(2):
                P.dma("pool", dst[:, :, h * 512:(h + 1) * 512],
                      D["w_out_c"][:, h * 512:(h + 1) * 512].re("(kc p) n -> p kc n", p=128))

        resid = [P.sbuf("resid%d" % i, [128, 1024], F32) for i in range(17)]
        identb = P.sbuf("identb", [128, 128], BF16)
        identf = P.sbuf("identf", [128, 128], F32)
        trif = P.sbuf("trif", [128, 128], F32)
        trib = P.sbuf("trib", [128, 128], BF16)
        smalls = {}
        for nm, shp in [("g0T", [128, 8]), ("g1T", [128, 8]), ("glaT", [128, 8]), ("wsT", [128, 8, 3]),
                        ("wdT", [128, 8, 31]), ("bdw", [128, 8]), ("lng", [128, 8]), ("lnb", [128, 8]),
                        ("binc", [128, 24])]:
            smalls[nm] = P.sbuf("c_" + nm, shp, F32)
            P.dma("sp", smalls[nm].all(), D[nm].all())
        ssq1 = P.sbuf("ssq1", [128, 17], F32)
        ssq1h = P.sbuf("ssq1h", [128, 17, 2], F32)
        rstd1 = P.sbuf("rstd1", [128, 17], F32)
        onesb = P.sbuf("onesb", [128, 128], BF16)
        onesdiv = P.sbuf("onesdiv", [128, 128], BF16)
        P.dma("sp", identb.all(), D["c_identb"].all())
        P.dma("sp", identf.all(), D["c_identf"].all())
        P.dma("sp", trif.all(), D["c_trif"].all())
        P.copy("dve", trib.all(), trif.all())
        trib16 = P.sbuf("trib16", [128, 128], BF16)
        neg16 = P.sbuf("neg16", [128, 1], BF16)
        P.ts("dve", trib16.all(), trif.all(), -1.0 / 16, ALU.mult)
        P.memset("dve", neg16.all(), -1.0 / 16)
        P.memset("dve", onesb.all(), 1.0)
        P.memset("dve", onesdiv.all(), 1.0 / 1024)
        P.memset("dve", ssq1h.all(), 1.0)

        banks = [P.psum("ps%d" % i, [128, 512], F32) for i in range(8)]
        ring = Ring(banks)

        def load_w(dst, src_view, ncols, q="pool"):
            c = 0
            while c < ncols:
                w = min(1024, ncols - c)
                P.dma(q, dst[:, :, c:c + w], src_view[:, c:c + w].re("(kc p) n -> p kc n", p=128))
                c += w

        def scale_rows(dst, gT, ncols, e="dve"):
            for kc in range(8):
                P.ts(e, dst[:, kc, 0:ncols], dst[:, kc, 0:ncols], gT[:, kc:kc + 1], ALU.mult)

        def rstd_from_ssq(dst, src, n, eps):
            P.act(dst, src, AF.Ln, scale=1.0 / n, bias=eps_t[eps][: dst.ap.shape[0], 0:1])
            P.act(dst, dst, AF.Exp, scale=-0.5)

        eps_t = {}
        for ev in (RMS_EPS, LN_EPS, 1.0):
            eps_t[ev] = P.sbuf("eps%g" % ev, [128, 1], F32)
            P.memset("dve", eps_t[ev].all(), float(ev))

        def transposes_to(dst_view_fn, src, ntok, nchunks, ident, dt_bf=True, e="act", gT=None):
            pb = ring.get()
            pv = pb.all().cast(BF16) if dt_bf else pb.all()
            for kc in range(nchunks):
                P.tr(pv[:, kc * ntok:(kc + 1) * ntok], src[:ntok, kc * 128:(kc + 1) * 128], ident[:ntok, :ntok],
                     inc=(kc == nchunks - 1))
            if gT is None:
                P.copy(e, dst_view_fn(), pv[:, 0:nchunks * ntok].re("p (k t) -> p k t", k=nchunks))
            else:
                P.tt("dve", dst_view_fn(), pv[:, 0:nchunks * ntok].re("p (k t) -> p k t", k=nchunks),
                     gT.all().re("p (k o) -> p k o", o=1).bc([128, nchunks, ntok]), ALU.mult)

        es_hT = ExitStack()
        hT = P.sbuf("hT", [128, 8, TOK + NS], BF16, es_hT)
        with ExitStack() as esA:
            WG = {}
            for nm, c0, w in (("a", 3072, 16), ("q", 0, 512), ("k", 512, 512), ("v0", 1024, 512), ("v1", 1536, 512),
                              ("g0", 2048, 512), ("g1", 2560, 512)):
                WG[nm] = (P.sbuf("Wg_" + nm, [128, 8, w], BF16, esA), c0, w)
            Wot = [P.sbuf("Wot%d" % n, [128, 8, 512], BF16, esA) for n in range(2)]
            wgu = P.sbuf("wgu", [32, 512], BF16, esA)
            delta = P.sbuf("delta", [128, 16, 16], F32, esA)
            P.dma("sp", delta.all(), D["c_delta"].all())
            P.dma("pool", wgu[0:17, :], D["wgu"].all())
            for nm in ("a", "q", "k", "v0", "v1", "g0", "g1"):
                wb, c0, w = WG[nm]
                load_w(wb, D["w_in_a"][:, c0:c0 + w], w)
            for n in range(2):
                load_w(Wot[n], D["w_out_a"][0:1024, n * 512:(n + 1) * 512], 512)
                scale_rows(Wot[n], smalls["glaT"], 512)

            ssqs = [P.sbuf("ssq%d" % i, [128, 1], F32, esA) for i in range(2)]
            rstds = [P.sbuf("rstd%d" % i, [128, 1], F32, esA) for i in range(2)]
            om = P.sbuf("om", [128, 1024], BF16, esA)
            omTf = P.sbuf("omTf", [128, 1024], BF16, esA)
            omT = omTf.all().re("p (k t) -> p k t", k=8)
            xns = [om, omTf]
            aaug = P.sbuf("aaug", [32, 128], BF16, esA)
            eb = P.sbuf("eb", [128, 512], F32, esA)
            qe = P.sbuf("qe", [128, 512], BF16, esA)
            sgh = P.sbuf("sgh", [128, 512], F32, esA)
            ossq = P.sbuf("ossq", [128, 4], F32, esA)
            rso = P.sbuf("rso", [128, 4], F32, esA)
            atok = eb
            qtok = qe
            SB = {"aT": P.sbuf("aT", [128, 4, 16], F32, esA), "qT": P.sbuf("qT", [128, 4, 16], BF16, esA),
                  "Qm": P.sbuf("Qm", [128, 4, 16, 16], BF16, esA), "vbfS": P.sbuf("vbfS", [128, 1024], BF16, esA),
                  "ksb": P.sbuf("ksb", [16, 512], F32, esA)}
            esAp = ExitStack()
            la = P.sbuf("la", [128, 512], BF16, esAp)
            enb = P.sbuf("enb", [128, 512], F32, esAp)
            scm = P.sbuf("scm", [128, 4, 128], BF16, esAp)
            S = P.sbuf("S", [128, 4, 256], F32, esAp)
            Sbf = P.sbuf("Sbf", [128, 4, 256], BF16, esAp)
            FBs = [{"qkT": P.sbuf("qkT%d" % i, [128, 8, 128], BF16, esAp), "ke": P.sbuf("ke%d" % i, [128, 512], BF16, esAp),
                    "vbf": P.sbuf("vbf%d" % i, [128, 1024], BF16, esAp), "dec": P.sbuf("dec%d" % i, [128, 4], F32, esAp)}
                   for i in range(2)]

            P.memset("dve", aaug.all(), 1.0)
            P.memset("dve", S.all(), 0.0)

            def tinfo(t):
                sample = (t == 16)
                return sample, (NS if sample else 128), (TOK if sample else t * 128)

            def proj_tok(ps, col0, ntok, nm):
                wb, _, ncols = WG[nm]
                for kc in range(8):
                    P.mm(ps[:ntok, :ncols], hT[:, kc, col0:col0 + ntok], wb[:, kc, :],
                         start=(kc == 0), stop=(kc == 7))

            for t in range(17):
                if t == 16:
                    P.dma("sp", resid[16][:NS, :], D["xs"].all())
                else:
                    P.dma("sp", resid[t].all(), D["x"][t * 128:(t + 1) * 128, :])
            for t in range(17):
                sample, ntok, col0 = tinfo(t)
                xt, xn, ssq, rstd = resid[t], xns[t % 2], ssqs[t % 2], rstds[t % 2]
                P.act(xn[:ntok, :], xt[:ntok, :], AF.Square, accum_out=ssq[:ntok, :])
                rstd_from_ssq(rstd[:ntok, :], ssq[:ntok, :], 1024, RMS_EPS)
                P.ts("dve", xn[:ntok, :], xt[:ntok, :], rstd[:ntok, :], ALU.mult)
                transposes_to(lambda: hT[:, :, col0:col0 + ntok], xn, ntok, 8, identb, e="mix", gT=smalls["g0T"])

            FR = {}

            def front(t):
                sample, ntok, col0 = tinfo(t)
                fb = FBs[t % 2]
                qkT, ke, vbf, dec = fb["qkT"], fb["ke"], fb["vbf"], fb["dec"]
                pa = ring.get()
                for kc in range(8):
                    P.mm(pa[:16, :ntok], WG["a"][0][:, kc, :], hT[:, kc, col0:col0 + ntok],
                         start=(kc == 0), stop=(kc == 7))
                P.copy("dve", aaug[0:16, :ntok], pa[:16, :ntok])
                yield 1.2
                pla = ring.get()
                P.mm(pla[:ntok, :], aaug[0:17, :ntok], wgu[0:17, :])
                pe1 = ring.get()
                P.act(pe1[:ntok, :], pla[:ntok, :], AF.Exp, scale=-1.0)
                P.act(la.all(), pe1.all(), AF.Ln, bias=eps_t[1.0][:, 0:1])
                yield 2.0
                pb = ring.get()
                P.mm(pb.all(), trib16.all(), la.all())
                pbl = ring.get()
                for h in range(4):
                    P.mm(pbl[:, h:h + 1], la[:, h * 128:(h + 1) * 128], neg16[:, 0:1], inc=(h == 3))
                P.act(eb.all(), pb.all(), AF.Exp)
                P.act(enb.all(), pb.all(), AF.Exp, scale=-1.0)
                P.act(dec.all(), pbl[:, 0:4], AF.Exp)
                yield 2.2
                pq = ring.get()
                proj_tok(pq, col0, ntok, "q")
                P.stt(qe.all(), pq.all(), 128 ** -0.5, eb.all(), ALU.mult, ALU.mult)
                yield 2.6
                pk = ring.get()
                proj_tok(pk, col0, ntok, "k")
                P.tt("dve", ke.all(), pk.all(), enb.all(), ALU.mult)
                yield 2.6
                tq = ring.get()
                tqv = tq.all().cast(BF16)
                for h in range(4):
                    P.tr(tqv[:, h * 128:(h + 1) * 128], qe[:, h * 128:(h + 1) * 128], identb.all(), inc=False)
                for h in range(4):
                    P.tr(tqv[:, 512 + h * 128:512 + (h + 1) * 128], ke[:, h * 128:(h + 1) * 128], identb.all(),
                         inc=(h == 3))
                P.copy("dve", qkT.all().re("p k t -> p (k t)"), tqv)
                yield 1.8
                for n in range(2):
                    pv = ring.get()
                    proj_tok(pv, col0, ntok, "v%d" % n)
                    P.copy("act", vbf[:ntok, n * 512:(n + 1) * 512], pv[:ntok, :])
                    yield 2.5

            def front_sample():
                t = 16
                sample, ntok, col0 = tinfo(t)
                vbf = SB["vbfS"]
                aT, qT, Qm = SB["aT"], SB["qT"], SB["Qm"]
                pa = ring.get()
                for kc in range(8):
                    P.mm(pa[:16, :ntok], WG["a"][0][:, kc, :], hT[:, kc, col0:col0 + ntok],
                         start=(kc == 0), stop=(kc == 7))
                P.copy("dve", aaug[0:16, :ntok], pa[:16, :ntok])
                pla = ring.get()
                P.mm(pla[:ntok, :], aaug[0:17, :ntok], wgu[0:17, :])
                pe1 = ring.get()
                P.act(pe1[:ntok, :], pla[:ntok, :], AF.Exp, scale=-1.0)
                pl1 = ring.get()
                P.act(pl1[:ntok, :], pe1[:ntok, :], AF.Ln, bias=eps_t[1.0][:ntok, 0:1])
                P.act(atok[:NS, :], pl1[:NS, :], AF.Exp, scale=-1.0 / 16)
                pq = ring.get()
                proj_tok(pq, col0, ntok, "q")
                P.ts("dve", qtok[:NS, :], pq[:NS, :], 128 ** -0.5, ALU.mult)
                yield 4.0
                pk = ring.get()
                proj_tok(pk, col0, ntok, "k")
                P.copy("dve", SB["ksb"].all(), pk[:NS, :])
                yield 2.5
                for n in range(2):
                    pv = ring.get()
                    proj_tok(pv, col0, ntok, "v%d" % n)
                    P.copy("act", vbf[:ntok, n * 512:(n + 1) * 512], pv[:ntok, :])
                    yield 2.5
                pt = ring.get()
                for h in range(4):
                    P.tr(pt[:, h * 16:(h + 1) * 16], atok[:NS, h * 128:(h + 1) * 128], identf[:16, :16], inc=(h == 3))
                P.copy("act", aT.all().re("p h s -> p (h s)"), pt[:, 0:64])
                pt2 = ring.get()
                pt2v = pt2.all().cast(BF16)
                for h in range(4):
                    P.tr(pt2v[:, h * 16:(h + 1) * 16], qtok[:NS, h * 128:(h + 1) * 128], identb[:16, :16], inc=(h == 3))
                P.copy("act", qT.all().re("p h s -> p (h s)"), pt2v[:, 0:64])
                yield 2.0

            def core(t):
                fb = FBs[t % 2]
                qkT, ke, vbf, dec = fb["qkT"], fb["ke"], fb["vbf"], fb["dec"]
                psc = ring.get()
                for h in range(4):
                    P.mm(psc[:, h * 128:(h + 1) * 128], qkT[:, 4 + h, :], qkT[:, h, :], inc=(h == 3))
                P.tt("dve", scm.all(), psc.all().re("p (h l) -> p h l", h=4),
                     trif.all().re("p (o l) -> p o l", o=1).bc([128, 4, 128]), ALU.mult)
                yield 1.5
                po = [ring.get(hold=True), ring.get(hold=True)]
                pos = []
                for h in range(4):
                    ov = po[h // 2][:, (h % 2) * 256:(h % 2 + 1) * 256]
                    P.mm(ov, scm[:, h, :], vbf[:, h * 256:(h + 1) * 256], start=True, stop=(t == 0))
                    if t > 0:
                        P.mm(ov, qkT[:, h, :], Sbf[:, h, :], start=False, stop=True)
                    pos.append(ov)
                FR["pos"] = pos
                yield 1.2
                for h2 in range(2):
                    pd = ring.get()
                    for hh in range(2):
                        h = h2 * 2 + hh
                        P.mm(pd[:, hh * 256:(hh + 1) * 256], ke[:, h * 128:(h + 1) * 128],
                             vbf[:, h * 256:(h + 1) * 256], inc=(hh == 1))
                    for hh in range(2):
                        h = h2 * 2 + hh
                        P.tt("dve", S[:, h, :], pd[:, hh * 256:(hh + 1) * 256], S[:, h, :], ALU.add)
                        P.act(S[:, h, :], S[:, h, :], AF.Copy, scale=dec[:, h:h + 1])
                        if t < 15:
                            P.copy("dve", Sbf[:, h, :], S[:, h, :])
                    yield 1.5
                if t == 15:
                    P.dma("sp", D["glap"].all().re("h d v -> d h v"), S.all())

            def core_sample():
                pk = SB["ksb"]
                vbf = SB["vbfS"]
                aT, qT, Qm, kms, s0b, snbf = SB["aT"], SB["qT"], SB["Qm"], SB["kms"], SB["s0b"], SB["snbf"]
                qaT, qkp, qk, osb = SB["qaT"], SB["qkp"], SB["qk"], SB["osb"]
                P.tt("dve", qaT.all(), qT.all(), aT.all(), ALU.mult)
                for h in range(4):
                    P.tt("dve", Qm[:, h, :, :], qaT[:, h, :].re("p (s o) -> p s o", o=1).bc([128, 16, 16]),
                         delta.all(), ALU.mult)
                P.tt("dve", qkp, qtok[:NS, :], pk[:NS, :], ALU.mult)
                P.op("dve", lambda: nc.vector.tensor_reduce(out=qk.all().ap, in_=qkp.re("s (h d) -> s h d", h=4).ap,
                                                            axis=mybir.AxisListType.X, op=ALU.add),
                     reads=[qkp], writes=[qk.all()])
                def load_s0(si):
                    P.dma("sp", s0b[si % 4].all(), D["sgla"][si].re("h d v -> d h v"))

                for si in range(3):
                    load_s0(si)
                pobs = [ring.get(hold=True) for _ in range(4)]
                kms2 = SB["kms2"]
                stg = {}

                def stage1(si):
                    km = kms2[0]
                    P.ts("dve", km.all(), pk[:NS, :], identf[:NS, si:si + 1], ALU.mult)
                    pds = [ring.get(), ring.get()]
                    for h in range(4):
                        P.mm(pds[h // 2][:, (h % 2) * 256:(h % 2 + 1) * 256], km[:, h * 128:(h + 1) * 128],
                             vbf[:NS, h * 256:(h + 1) * 256])
                    stg[si] = pds

                stage1(0)
                for si in range(NS):
                    if si + 3 < NS:
                        load_s0(si + 3)
                    if si + 1 < NS:
                        stage1(si + 1)
                    pds = stg.pop(si)
                    s0 = s0b[si % 4]
                    sn = s0
                    s0f = snbf[si % 2]
                    P.copy("act", s0f.all(), s0.all())
                    for h in range(4):
                        P.stt(sn[:, h, :], s0[:, h, :], aT[:, h, si:si + 1],
                              pds[h // 2][:, (h % 2) * 256:(h % 2 + 1) * 256], ALU.mult, ALU.add)
                        P.mm(pobs[h][:NS, 0:256], Qm[:, h, si, :], s0f[:, h, :], start=(si == 0), stop=(si == NS - 1))
                    P.dma("pool", D["glas"][si].re("h d v -> d h v"), sn.all())
                pos = []
                for h in range(4):
                    P.stt(osb[:, h * 256:(h + 1) * 256], vbf[:NS, h * 256:(h + 1) * 256], qk[:, h:h + 1],
                          pobs[h][:NS, 0:256], ALU.mult, ALU.add)
                    ring.release(pobs[h])
                    pos.append(osb[:, h * 256:(h + 1) * 256])
                return pos

            def tail(t, pos):
                sample, ntok, col0 = tinfo(t)
                xt = resid[t]
                pj = ring.get()
                for h in range(4):
                    P.act(pj[:ntok, 0:256], pos[h][:ntok, :], AF.Square, accum_out=ossq[:ntok, h:h + 1])
                rstd_from_ssq(rso[:ntok, :], ossq[:ntok, :], 256, RMS_EPS)
                for n in range(2):
                    pg = ring.get()
                    proj_tok(pg, col0, ntok, "g%d" % n)
                    P.act(sgh[:ntok, :], pg[:ntok, :], AF.Silu)
                    for hh in range(2):
                        h = n * 2 + hh
                        P.stt(om[:ntok, h * 256:(h + 1) * 256], pos[h][:ntok, :], rso[:ntok, h:h + 1],
                              sgh[:ntok, hh * 256:(hh + 1) * 256], ALU.mult, ALU.mult)
                    yield 3.0
                for b_ in set(id(v.buf) for v in pos):
                    ring.held.discard(b_)
                transposes_to(lambda: omT[:, :, :ntok], om, ntok, 8, identb, e="dve")
                yield 1.5
                for n in range(2):
                    pout = ring.get()
                    for kc in range(8):
                        P.mm(pout[:ntok, :], omT[:, kc, :ntok], Wot[n][:, kc, :],
                             start=(kc == 0), stop=(kc == 7))
                    P.tt("dve", xt[:ntok, n * 512:(n + 1) * 512], pout[:ntok, :],
                         xt[:ntok, n * 512:(n + 1) * 512], ALU.add)
                    yield 2.5

            def merge(*gens, skew=()):
                gens = [g for g in gens if g is not None]
                clk = [0.0] * len(gens)
                for i, v in enumerate(skew):
                    if i < len(clk):
                        clk[i] = v
                alive = list(range(len(gens)))
                while alive:
                    i = min(alive, key=lambda k: clk[k])
                    try:
                        dt = next(gens[i])
                        clk[i] += dt if dt else 1.0
                    except StopIteration:
                        alive.remove(i)

            def body(t):
                yield from core(t)
                yield from tail(t, FR["pos"])

            merge(front(0))
            for t in range(16):
                if t == 1:
                    precast_B()
                if t == 5:
                    precast_C()
                merge(front(t + 1) if t + 1 < 16 else front_sample(), body(t))
            P.barrier()
            esAp.close()
            with ExitStack() as esAs:
                SB.update({"kms": None, "kms2": [P.sbuf("kms2%d" % i, [16, 512], BF16, esAs) for i in range(1)],
                           "s0b": [P.sbuf("s0b%d" % i, [128, 4, 256], F32, esAs) for i in range(4)],
                           "snbf": [P.sbuf("snbf%d" % i, [128, 4, 256], BF16, esAs) for i in range(2)],
                           "qaT": P.sbuf("qaT", [128, 4, 16], BF16, esAs), "qkp": sgh[:NS, :],
                           "qk": P.sbuf("qk", [16, 4], F32, esAs)})
                SB["osb"] = SB["s0b"][0].all().re("p h v -> p (h v)")[:NS, :]
                pos = core_sample()
                merge(tail(16, pos))
                P.barrier()

        if stop_after == "A":
            r_ = _finish(nc, P, D, resid, debug_resid)
            es_hT.close()
            return r_

        NB = 256
        with ExitStack() as esB:
            WscB = [P.sbuf("Wsc%d" % c, [128, 8, 4, 128], BF16, esB) for c in range(8)]
            Wob = P.sbuf("Wob", [128, 8, 1024], BF16, esB)
            for c in range(8):
                P.dma("sp", WscB[c].all().re("p kc g n -> p (kc g n)"), WSCD[c])
            P.dma("sp", Wob.all().re("p kc n -> p (kc n)"), WOBD.all())
            dg3 = P.sbuf("dg3", [128, 8, 3, 128], BF16, esB)
            for c in range(8):
                for j in range(3):
                    P.ts("dve", dg3[:, c, j, :], identb.all(), smalls["wsT"][:, c, j:j + 1], ALU.mult)
            hbs = [P.sbuf("hbs%d" % i, [128, NB], F32, esB) for i in range(2)]
            szb = [P.sbuf("szb%d" % i, [128, NB], F32, esB) for i in range(2)]
            yT = P.sbuf("yT", [128, 8, NB], BF16, esB)

            def phaseB_tile(T, sample, B, hook=None):
                N = NS if sample else NB
                col0 = TOK if sample else T * NB
                last = (not sample) and (T == TOK // NB - 1)
                if sample:
                    sscb, ue, usf, usT = B["sscb"], B["ue"], B["usf"], B["usT"]
                    P.dma("sp", D["scs"][:, 0, :], D["ssc"][:, 1, :])
                    pt = ring.get()
                    for j in range(2):
                        P.dma("sp", sscb.all(), D["ssc"][:, j, :])
                        for c in range(8):
                            jc = j * 8 + c
                            P.tr(pt[:, jc * 16:(jc + 1) * 16], sscb[:, c * 128:(c + 1) * 128], identf[:16, :16],
                                 inc=True)
                    for j in range(2):
                        P.copy("act", ue[:, :, j, :], pt[:, j * 128:(j + 1) * 128].re("p (c s) -> p c s", c=8))
                else:
                    ub, ust, ustT = B["ub"], B["ust"], B["ustT"]
                for c in range(8):
                    ps = []
                    for g in range(4):
                        pb_ = ring.get()
                        for kc in range(8):
                            P.mm(pb_[:, :N], WscB[c][:, kc, g, :], hT[:, kc, col0:col0 + N],
                                 start=(kc == 0), stop=(kc == 7))
                        ps.append(pb_)
                    phb, pgb, pgc, pzb = ps
                    hb = hbs[c % 2]
                    sz = szb[c % 2]
                    P.copy("act", hb[:, :N], phb[:, :N])
                    py = ring.get()
                    if not sample:
                        if T > 0:
                            P.copy("dve", ub[c][:, 0:2], ub[c][:, NB:NB + 2])
                        P.tt("dve", ub[c][:, 2:NB + 2], pgc[:, :N], hb[:, :N], ALU.mult)
                        if last:
                            P.tt("dve", ust[:, :, c], pgc[:, NB - 2:NB], hb[:, NB - 2:NB], ALU.mult)
                        for j in range(3):
                            P.mm(py[:, :N], dg3[:, c, j, :], ub[c][:, j:j + NB], start=(j == 0), stop=(j == 2))
                    else:
                        P.tt("dve", usf[:, c, :], pgc[:, :N], hb[:, :N], ALU.mult)
                        P.copy("dve", ue[:, c, 2, :], usf[:, c, :])
                        for j in range(3):
                            P.mm(py[:, :N], dg3[:, c, j, :], ue[:, c, j, :], start=(j == 0), stop=(j == 2))
                    P.act(sz[:, :N], pzb[:, :N], AF.Silu)
                    P.tt("dve", sz[:, :N], pgb[:, :N], sz[:, :N], ALU.mult)
                    P.tt("dve", yT[:, c, :N], py[:, :N], sz[:, :N], ALU.mult)
                    if hook is not None:
                        hook(c)
                nsub = 1 if sample else NB // 128
                for s in range(nsub):
                    ntok = NS if sample else 128
                    ti = 16 if sample else T * nsub + s
                    for n in range(2):
                        pout = ring.get()
                        for c in range(8):
                            P.mm(pout[:ntok, :], yT[:, c, s * 128:s * 128 + ntok], Wob[:, c, n * 512:(n + 1) * 512],
                                 start=(c == 0), stop=(c == 7))
                        P.tt("dve", resid[ti][:ntok, n * 512:(n + 1) * 512], pout[:ntok, :],
                             resid[ti][:ntok, n * 512:(n + 1) * 512], ALU.add)
                    for n in range(2):
                        P.act(ring.get()[:ntok, :], resid[ti][:ntok, n * 512:(n + 1) * 512], AF.Square,
                              accum_out=ssq1h[:ntok, ti, n:n + 1])
                if last:
                    pt = ring.get()
                    P.tr(pt[:16, 0:128], ust.all().re("p j c -> p (j c)"), identf.all())
                    P.copy("act", ustT.all(), pt[:16, 0:128])
                    P.dma("sp", D["scp"].all().re("j (c p) -> (j c) p", p=128), ustT.all())
                if sample:
                    for half in range(2):
                        pt = ring.get()
                        for cc in range(4):
                            c = half * 4 + cc
                            P.tr(pt[:16, cc * 128:(cc + 1) * 128], usf[:, c, :], identf.all(), inc=(cc == 3))
                        P.copy("act", usT[:, half * 512:(half + 1) * 512], pt[:16, :])
                    P.dma("sp", D["scs"][:, 1, :], usT.all())

            with ExitStack() as esBs:
                B = {"sscb": P.sbuf("sscb", [16, 1024], F32, esBs), "ue": P.sbuf("ue", [128, 8, 3, 16], BF16, esBs),
                     "usf": P.sbuf("usf", [128, 8, 16], F32, esBs)}
                B["usT"] = B["sscb"]
                phaseB_tile(0, True, B)
                P.barrier()
            with ExitStack() as esBp:
                B = {"ub": [P.sbuf("ub%d" % c, [128, NB + 2], BF16, esBp) for c in range(8)],
                     "ust": P.sbuf("ust", [128, 2, 8], F32, esBp), "ustT": P.sbuf("ustT", [16, 128], F32, esBp)}
                for c in range(8):
                    P.memset("dve", B["ub"][c][:, 0:2], 0.0)
                dgb = P.sbuf("dgb", [128, 16, 128], BF16, esBp)
                def dg_hook(T):
                    def hook(c):
                        cc = T
                        dsrc = D["dgd"][cc].re("p (j q) -> p j q", j=31)
                        j0, j1 = c * 4, min(31, c * 4 + 4)
                        base = 0 if c < 4 else 16
                        for j in range(j0, j1):
                            P.ts("dve", dgb[:, j - base, :], identb.all(), smalls["wdT"][:, cc, j:j + 1], ALU.mult)
                        if c == 3:
                            P.dma("sp", dsrc[:, 0:16, :], dgb[:, 0:16, :])
                        if c == 7:
                            P.dma("sp", dsrc[:, 16:31, :], dgb[:, 0:15, :])
                    return hook

                for T in range(TOK // NB):
                    phaseB_tile(T, False, B, hook=dg_hook(T))
                P.barrier()

        if stop_after == "B":
            r_ = _finish(nc, P, D, resid, debug_resid)
            es_hT.close()
            return r_
        es_hT.close()

        with ExitStack() as esC:
            WicB = [P.sbuf("Wic%d" % c, [128, 8, 3, 128], BF16, esC) for c in range(8)]
            Woc = P.sbuf("Woc", [128, 8, 1024], BF16, esC)
            brows = P.sbuf("brows", [64, 1024], BF16, esC)
            boutb = brows
            P.dma("pool", brows[0:1, :], D["bout"].all())
            bh = P.sbuf("bh", [128, 24], F32, esC)
            P.ts("dve", bh.all(), smalls["binc"].all(), 0.5, ALU.mult)
            P.dma("pool", brows[32:33, :], D["bincrow"][:, 0:1024])
            P.ts("dve", brows[32:33, :], brows[32:33, :], 0.5, ALU.mult)
            for c in range(8):
                P.dma("sp", WicB[c].all().re("p kc g n -> p (kc g n)"), WICD[c])
                P.ts("dve", WicB[c][:, :, 0, :], WicB[c][:, :, 0, :], 0.5, ALU.mult)
            P.dma("sp", Woc.all().re("p kc n -> p (kc n)"), WOCD.all())
            fgbc = P.sbuf("fgbc", [128, 1024], F32, esC)
            P.dma("sp", fgbc.all(), D["fgbc"].all())
            NT = 256
            dgr = [P.sbuf("dgr%d" % i, [128, 16, 128], BF16, esC) for i in range(4)]
            dg_built = [True]
            P.tt("dve", ssq1.all(), ssq1h[:, :, 0], ssq1h[:, :, 1], ALU.add)
            rstd_from_ssq(rstd1.all(), ssq1.all(), 1024, RMS_EPS)
            ufl = P.sbuf("ufl", [128, 8, 30], F32, esC)
            ssq2 = P.sbuf("ssq2", [128, 1], F32, esC)
            ssq2h = P.sbuf("ssq2h", [128, 2], F32, esC)
            rstd2 = P.sbuf("rstd2", [128, 1], F32, esC)
            xnC = P.sbuf("xnC", [128, 1024], BF16, esC)
            dgi = [0]

            FRC = {}

            def cinfo(Tc, sample):
                N = NS if sample else NT
                nsub = 1 if sample else NT // 128
                ntok = NS if sample else 128
                last = (not sample) and (Tc == TOK // NT - 1)
                return N, nsub, ntok, last

            def frontC(Tc, sample, C):
                N, nsub, ntok, last = cinfo(Tc, sample)
                b = Tc % 2
                h1T, yc = C["h1T"][b], C["yc"][b]
                sgb, yst = C["sgb"], C["yst"]
                for s in range(nsub):
                    ti = 16 if sample else Tc * nsub + s
                    P.ts("dve", xnC[:ntok, :], resid[ti][:ntok, :], rstd1[:ntok, ti:ti + 1], ALU.mult)
                    transposes_to(lambda: h1T[:, :, s * 128:s * 128 + ntok], xnC, ntok, 8, identb, e="dve",
                                  gT=smalls["g1T"])
                    yield 2.5
                if sample:
                    sccb, bT, ucs, ufs = C["sccb"], C["bT"], C["ucs"], C["ufs"]
                    P.dma("sp", D["ccs"][:, 0:29, :], D["scc"][:, 1:30, :])
                    for q in range(4):
                        P.dma("sp", sccb.all(), D["scc"].all().re("s j c -> (s j) c")[q * 120:(q + 1) * 120, :])
                        for c in range(8):
                            pt = ring.get()
                            P.tr(pt[:, 0:120], sccb[:, c * 128:(c + 1) * 128], identf[:120, :120])
                            P.copy("act", bT[:, c, q * 120:(q + 1) * 120], pt[:, 0:120])
                else:
                    uc = C["uc"]
                pst = ring.get(hold=True)

                def proj_c(c):
                    pab = ring.get(hold=True)
                    for n0 in range(0, N, 128):
                        n1 = min(N, n0 + 128)
                        P.mm(pab[:, n0:n1], brows[32:33, c * 128:(c + 1) * 128], onesb[32:33, 0:n1 - n0],
                             start=(n0 == 0), stop=False)
                    for kc in range(8):
                        P.mm(pab[:, 0:N], WicB[c][:, kc, 0, :], h1T[:, kc, :N], start=False, stop=(kc == 7))
                    for kc in range(8):
                        P.mm(pab[:, 256:256 + N], WicB[c][:, kc, 1, :], h1T[:, kc, :N],
                             start=(kc == 0), stop=(kc == 7))
                    return pab

                nxt = proj_c(0)
                pend = []
                for c in range(8):
                    pab = nxt
                    pa, pag = pab[:, 0:N], pab[:, 256:256 + N]
                    sgc = sgb[c % 2]
                    P.act(sgc[:, :N], pag, AF.Tanh, scale=0.5, bias=bh[:, 8 + c:9 + c])
                    dgh = [dgr[(dgi[0] + k) % 4] for k in range(2)]
                    dgi[0] += 2
                    dsrc = D["dgd"][c].re("p (j q) -> p j q", j=31)
                    for k, (j0, j1) in enumerate(((0, 16), (16, 31))):
                        if not dg_built[0]:
                            for j in range(j0, j1):
                                P.ts("dve", dgh[k][:, j - j0, :], identb.all(), smalls["wdT"][:, c, j:j + 1], ALU.mult)
                            P.dma("act", dsrc[:, j0:j1, :], dgh[k][:, 0:j1 - j0, :])
                        else:
                            P.dma("sp", dgh[k][:, 0:j1 - j0, :], dsrc[:, j0:j1, :])

                    def dgt_tap(j):
                        return dgh[j // 16][:, j % 16, :]
                    if not sample:
                        P.stt(uc[c][:, 30:30 + NT], sgc[:, :N], 1.0, pa, ALU.add, ALU.mult)
                        if last:
                            P.stt(ufl[:, c, :], sgc[:, NT - 30:NT], 1.0, pab[:, NT - 30:NT], ALU.add, ALU.mult)
                    else:
                        P.stt(ufs[:, c, :], sgc[:, :N], 1.0, pa, ALU.add, ALU.mult)
                        P.copy("dve", ucs[:, c, :], ufs[:, c, :])
                    ring.release(pab)
                    if c + 1 < 8:
                        nxt = proj_c(c + 1)
                    yield 1.7
                    py = ring.get()
                    if not sample:
                        for j in range(31):
                            P.mm(py[:, :N], dgt_tap(j), uc[c][:, j:j + NT], start=(j == 0), stop=(j == 30))
                        if not last:
                            P.copy("pool", uc[c][:, 0:30], uc[c][:, NT:NT + 30])
                    else:
                        for j in range(30):
                            P.mm(py[:, :N], dgt_tap(j), bT[:, c, :].re("p (s j) -> p j s", j=30)[:, j, :],
                                 start=(j == 0), stop=False)
                        P.mm(py[:, :N], dgt_tap(30), ucs[:, c, :], start=False, stop=True)
                    yield 3.4
                    while pend:
                        pend.pop(0)()
                    ys = yst[c % 2]
                    P.act(yc[:, c, :N], py[:, :N], AF.Identity, bias=smalls["bdw"][:, c:c + 1])
                    P.act(ys[:, 1, :], py[:, :N], AF.Square, bias=smalls["bdw"][:, c:c + 1])
                    P.copy("pool", ys[:, 0, :], yc[:, c, :N])
                    pend.append(lambda ys=ys, c=c: P.mm(pst[:, 0:2 * N], onesdiv.all(), ys.all().re("p a n -> p (a n)"),
                                                        start=(c == 0), stop=(c == 7)))
                    yield 1.3
                while pend:
                    pend.pop(0)()
                dg_built[0] = True
                FRC[(Tc, sample)] = pst

            def backC(Tc, sample, C):
                N, nsub, ntok, last = cinfo(Tc, sample)
                b = Tc % 2
                h1T, yc = C["h1T"][b], C["yc"][b]
                msq, var, dd, t2, sl, szc, ycT = C["msq"], C["var"], C["dd"], C["t2"], C["sl"], C["szc"], C["ycT"]
                pst = FRC.pop((Tc, sample))
                mean, ex2 = pst[:, 0:N], pst[:, N:2 * N]
                P.act(msq[:, :N], mean, AF.Square)
                P.tt("dve", var[:, :N], ex2, msq[:, :N], ALU.subtract)
                P.act(var[:, :N], var[:, :N], AF.Ln, bias=eps_t[LN_EPS][:, 0:1])
                P.act(var[:, :N], var[:, :N], AF.Exp, scale=-0.5)
                prn = ring.get(hold=True)
                prs, nmr = prn[:, 0:N], prn[:, 256:256 + N]
                P.copy("act", prs, var[:, :N])
                P.stt(nmr, mean, -1.0, var[:, :N], ALU.mult, ALU.mult)
                ring.release(pst)
                yield 3.0
                for c in range(8):
                    P.tt("dve", dd[c % 2][:, :N], yc[:, c, :N], prs, ALU.mult)
                    P.tt("dve", t2[c % 2][:, :N], dd[c % 2][:, :N], nmr, ALU.add)
                    pz = ring.get()
                    for kc in range(8):
                        P.mm(pz[:, :N], WicB[c][:, kc, 2, :], h1T[:, kc, :N],
                             start=(kc == 0), stop=(kc == 7))
                    P.act(sl[c % 2][:, :N], t2[c % 2][:, :N], AF.Silu, scale=smalls["lng"][:, c:c + 1],
                          bias=smalls["lnb"][:, c:c + 1])
                    P.act(szc[c % 2][:, :N], pz[:, :N], AF.Silu, bias=smalls["binc"][:, 16 + c:17 + c])
                    P.tt("pool", ycT[:, c, :N], sl[c % 2][:, :N], szc[c % 2][:, :N], ALU.mult)
                    yield 2.6
                ring.release(prn)
                for s in range(nsub):
                    ti = 16 if sample else Tc * nsub + s
                    for n in range(2):
                        pout = ring.get()
                        P.mm(pout[:ntok, :], onesb[0:1, :ntok], boutb[0:1, n * 512:(n + 1) * 512], start=True, stop=False)
                        for c in range(8):
                            P.mm(pout[:ntok, :], ycT[:, c, s * 128:s * 128 + ntok], Woc[:, c, n * 512:(n + 1) * 512],
                                 start=False, stop=(c == 7))
                        P.tt("dve", resid[ti][:ntok, n * 512:(n + 1) * 512], pout[:ntok, :],
                             resid[ti][:ntok, n * 512:(n + 1) * 512], ALU.add)
                        yield 2.6
                    for n in range(2):
                        P.act(ring.get()[:ntok, :], resid[ti][:ntok, n * 512:(n + 1) * 512], AF.Square,
                              accum_out=ssq2h[:ntok, n:n + 1])
                    P.tt("dve", ssq2[:ntok, :], ssq2h[:ntok, 0:1], ssq2h[:ntok, 1:2], ALU.add)
                    rstd_from_ssq(rstd2[:ntok, :], ssq2[:ntok, :], 1024, RMS_EPS)
                    P.stt(resid[ti][:ntok, :], resid[ti][:ntok, :], rstd2[:ntok, :], fgbc[:ntok, :], ALU.mult, ALU.mult)
                    if sample:
                        P.dma("sp", D["ys"].all(), resid[ti][:ntok, :])
                    else:
                        P.dma("sp", D["y"][ti * 128:(ti + 1) * 128, :], resid[ti].all())
                    yield 4.0
                if sample:
                    ufs, usTc = C["ufs"], C["usTc"]
                    for half in range(2):
                        pt = ring.get()
                        for cc in range(4):
                            c = half * 4 + cc
                            P.tr(pt[:16, cc * 128:(cc + 1) * 128], ufs[:, c, :], identf.all(), inc=(cc == 3))
                        P.copy("act", usTc[:, half * 512:(half + 1) * 512], pt[:16, :])
                    P.dma("sp", D["ccs"][:, 29, :], usTc.all())

            def work_bufs(esx, N, nb):
                return {"h1T": [P.sbuf("h1T%d" % i, [128, 8, N], BF16, esx) for i in range(nb)],
                        "yc": [P.sbuf("yc%d" % i, [128, 8, N], F32, esx) for i in range(nb)],
                        "sgb": [P.sbuf("sgb%d" % i, [128, N], F32, esx) for i in range(2)],
                        "yst": [P.sbuf("yst%d" % i, [128, 2, N], BF16, esx) for i in range(2)],
                        "msq": P.sbuf("msq", [128, N], F32, esx), "var": P.sbuf("var", [128, N], F32, esx),
                        "dd": [P.sbuf("dd%d" % i, [128, N], F32, esx) for i in range(2)],
                        "t2": [P.sbuf("t2%d" % i, [128, N], F32, esx) for i in range(2)],
                        "sl": [P.sbuf("sl%d" % i, [128, N], F32, esx) for i in range(2)],
                        "szc": [P.sbuf("szc%d" % i, [128, N], F32, esx) for i in range(2)],
                        "ycT": P.sbuf("ycT", [128, 8, N], BF16, esx)}

            MERGE_W = (1, 1)

            def mergeC(*gens):
                gens = [g for g in gens if g is not None]
                w = list(MERGE_W[:len(gens)]) if len(gens) > 1 else [1]
                while gens:
                    for gi, g in enumerate(list(gens)):
                        for _ in range(w[gi] if gi < len(w) else 1):
                            try:
                                next(g)
                            except StopIteration:
                                if g in gens:
                                    gens.remove(g)
                                break

            with ExitStack() as esCs:
                C = work_bufs(esCs, NS, 1)
                C.update({"sccb": P.sbuf("sccb", [120, 1024], F32, esCs), "bT": P.sbuf("bT", [128, 8, 480], BF16, esCs),
                          "ucs": P.sbuf("ucs", [128, 8, 16], BF16, esCs), "ufs": P.sbuf("ufs", [128, 8, 16], F32, esCs),
                          "usTc": P.sbuf("usTc", [16, 1024], F32, esCs)})
                mergeC(frontC(0, True, C))
                mergeC(backC(0, True, C))
                P.barrier()
            with ExitStack() as esCp:
                C = work_bufs(esCp, NT, 2)
                C["uc"] = [P.sbuf("uc%d" % c, [128, 30 + NT], BF16, esCp) for c in range(8)]
                for c in range(8):
                    P.memset("dve", C["uc"][c][:, 0:30], 0.0)
                nT = TOK // NT
                mergeC(frontC(0, False, C))
                for Tc in range(nT):
                    mergeC(frontC(Tc + 1, False, C) if Tc + 1 < nT else None, backC(Tc, False, C))
                P.barrier()
            with ExitStack() as esCe:
                ccpb = P.sbuf("ccpb", [30, 1024], F32, esCe)
                for half in range(2):
                    pt = ring.get()
                    for cc in range(4):
                        c = half * 4 + cc
                        P.tr(pt[:30, cc * 128:(cc + 1) * 128], ufl[:, c, :], identf.all(), inc=(cc == 3))
                    P.copy("act", ccpb[:, half * 512:(half + 1) * 512], pt[:30, :])
                P.dma("sp", D["ccp"].all(), ccpb.all())
                P.barrier()
        return _finish(nc, P, D, resid, debug_resid)


def _finish(nc, P, D, resid, debug_resid):
    if debug_resid:
        for i in range(17):
            P.dma("sp", D["dbg"][i], resid[i].all())
    P.barrier()
    print("ninstr", P.ninstr, "nsem", len(P.semobj))
    return nc


def _consts():
    identf = np.eye(128, dtype=np.float32)
    identb = identf.astype(ml_dtypes.bfloat16)
    trif = np.triu(np.ones((128, 128), dtype=np.float32))
    delta = np.broadcast_to(np.eye(16, dtype=np.float32)[None], (128, 16, 16)).copy()
    return {"c_identb": identb, "c_identf": identf, "c_trif": trif, "c_delta": delta}


def _fm(v, n):
    return np.ascontiguousarray(np.asarray(v, dtype=np.float32).reshape(n, 128).T)


def make_in_maps(inp):
    f = lambda k: np.asarray(inp[k], dtype=np.float32)
    shared = {
        "w_in_a": np.ascontiguousarray(f("w_in_a")[0]),
        "w_out_a": np.ascontiguousarray(f("w_out_a")[0]),
        "w_in_c": np.ascontiguousarray(f("w_in_c")[0]),
        "w_out_c": np.ascontiguousarray(f("w_out_c")[0]),
        "wgu": np.ascontiguousarray(np.concatenate([f("w_gate_up")[0], f("b_gate_up")[0][None, :]], axis=0)),
        "g0T": _fm(f("norm_g")[0], 8),
        "g1T": _fm(f("norm_g")[1], 8),
        "glaT": _fm(np.tile(f("gla_norm_g")[0], 4), 8),
        "wsT": np.ascontiguousarray(f("w_sconv")[0].T.reshape(8, 128, 3).transpose(1, 0, 2)),
        "wdT": np.ascontiguousarray(f("w_dwconv")[0].T.reshape(8, 128, 31).transpose(1, 0, 2)),
        "bdw": _fm(f("b_dwconv")[0], 8),
        "lng": _fm(f("ln_g")[0], 8),
        "lnb": _fm(f("ln_b")[0], 8),
        "binc": _fm(f("b_in_c")[0], 24),
        "bout": np.ascontiguousarray(f("b_out_c")[0][None, :]),
        "bincrow": np.ascontiguousarray(f("b_in_c")[0][None, :]),
        "fgbc": np.ascontiguousarray(np.broadcast_to(f("final_norm_g")[None, :], (128, 1024))),
    }
    shared.update(_consts())
    xp, xs = f("x_prompt"), f("x_sample")
    sg, ss, sc = f("state_gla"), f("state_sconv"), f("state_cconv")
    maps = []
    for b in range(NCORES):
        m = dict(shared)
        sl = slice(b * NS, (b + 1) * NS)
        m["x"] = np.ascontiguousarray(xp[b])
        m["xs"] = np.ascontiguousarray(xs[sl, 0, :])
        m["sgla"] = np.ascontiguousarray(sg[0, sl])
        m["ssc"] = np.ascontiguousarray(ss[0, sl])
        m["scc"] = np.ascontiguousarray(sc[0, sl])
        maps.append(m)
    return maps


_NC_CACHE = {}


def kernel(**inputs):
    if "nc" not in _NC_CACHE:
        _NC_CACHE["nc"] = build_nc()
    nc = _NC_CACHE["nc"]
    maps = make_in_maps(inputs)
    res = run_bass_kernel_spmd(nc, maps, core_ids=list(range(NCORES)))
    R = res.results
    y_prompt = np.stack([R[b]["y"] for b in range(NCORES)], axis=0)
    y_sample = np.concatenate([R[b]["ys"] for b in range(NCORES)], axis=0)[:, None, :]
    gla_p = np.stack([R[b]["glap"] for b in range(NCORES)], axis=0)[None]
    sconv_p = np.stack([R[b]["scp"] for b in range(NCORES)], axis=0)[None]
    cconv_p = np.stack([R[b]["ccp"] for b in range(NCORES)], axis=0)[None]
    gla_s = np.concatenate([R[b]["glas"] for b in range(NCORES)], axis=0)[None]
    sconv_s = np.concatenate([R[b]["scs"] for b in range(NCORES)], axis=0)[None]
    cconv_s = np.concatenate([R[b]["ccs"] for b in range(NCORES)], axis=0)[None]
    outs = (y_prompt, y_sample, gla_p, sconv_p, cconv_p, gla_s, sconv_s, cconv_s)
    return tuple(np.ascontiguousarray(o, dtype=np.float32) for o in outs)
```

```python
import numpy as np
import ml_dtypes
from contextlib import ExitStack
import concourse.bass as bass
import concourse.mybir as mybir
from concourse.bass_utils import run_bass_kernel_spmd

F32 = mybir.dt.float32
BF16 = mybir.dt.bfloat16
AF = mybir.ActivationFunctionType
ALU = mybir.AluOpType

SAME_ENGINE_SYNC = True
RMS_EPS = 1e-6
LN_EPS = 1e-5
NCORES = 8
NS = 16
TOK = 2048
STOP_AFTER = None


class Buf:
    def __init__(self, name, t):
        self.name = name
        self.t = t
        self.writes = {}
        self.reads = {}
        self.dsem = None
        self.dcnt = 0

    def __getitem__(self, idx):
        return View(self, self.t[idx])

    def all(self):
        return View(self, self.t[:])


class View:
    def __init__(self, buf, ap):
        self.buf = buf
        self.ap = ap

    def __getitem__(self, idx):
        return View(self.buf, self.ap[idx])

    def re(self, pat, **kw):
        return View(self.buf, self.ap.rearrange(pat, **kw))

    def bc(self, shape):
        return View(self.buf, self.ap.broadcast_to(list(shape)))

    def cast(self, dt):
        return View(self.buf, self.ap.bitcast(dt))


def _ap(v):
    return v.ap if isinstance(v, View) else v


class Ring:
    def __init__(self, bufs):
        self.bufs = bufs
        self.i = 0
        self.held = set()

    def get(self, hold=False):
        for _ in range(len(self.bufs) + 1):
            b = self.bufs[self.i]
            self.i = (self.i + 1) % len(self.bufs)
            if id(b) not in self.held:
                if hold:
                    self.held.add(id(b))
                return b
        raise RuntimeError("ring exhausted")

    def release(self, b):
        self.held.discard(id(b))


class Prog:
    ENG = ("pe", "dve", "act", "pool", "sp")

    def __init__(self, nc, es):
        self.nc = nc
        self.es = es
        self.eng = {"pe": nc.tensor, "dve": nc.vector, "act": nc.scalar, "pool": nc.gpsimd, "sp": nc.sync}
        self.semobj = {}
        self.cnt = {}
        for k in self.ENG:
            self.semobj[k] = es.enter_context(nc.semaphore("s_" + k))
            self.cnt[k] = 0
        self.dcnts = {}
        self.known = {k: {} for k in self.ENG}
        self.nbuf = 0
        self.ninstr = {k: 0 for k in self.ENG}

    def sbuf(self, name, shape, dt, es=None):
        self.nsb = getattr(self, "nsb", 0) + 1
        t = (es or self.es).enter_context(self.nc.sbuf_tensor("sb%d_%s" % (self.nsb, name), list(shape), dt))
        return Buf(name, t)

    def psum(self, name, shape, dt):
        t = self.es.enter_context(self.nc.psum_tensor(name, list(shape), dt))
        return Buf(name, t)

    def dram(self, name, ap):
        b = Buf(name, ap)
        b.is_dram = True
        return b

    def _emit_waits(self, e, deps):
        for k, v in deps.items():
            if self.known[e].get(k, 0) >= v:
                continue
            self.eng[e].wait_ge(self.semobj[k], v)
            self.ninstr[e] += 1
            self.known[e][k] = v

    def _deps(self, e, reads, writes):
        deps = {}

        def merge(d, skip_self):
            for k, v in d.items():
                if k == e and skip_self:
                    continue
                if deps.get(k, 0) < v:
                    deps[k] = v

        for r in reads:
            merge(r.buf.writes, not SAME_ENGINE_SYNC)
        skip_w = (e == "pe") or (not SAME_ENGINE_SYNC)
        for w in writes:
            merge(w.buf.writes, skip_w)
            merge(w.buf.reads, skip_w)
        return deps

    def _record(self, key, val, reads, writes):
        for r in reads:
            if r.buf.reads.get(key, 0) < val:
                r.buf.reads[key] = val
        for w in writes:
            if w.buf.reads:
                w.buf.reads = {}
                w.buf.writes = {}
            w.buf.writes[key] = val

    def op(self, e, fn, reads=(), writes=(), inc=True):
        reads = [r for r in reads if isinstance(r, View)]
        writes = [w for w in writes if isinstance(w, View)]
        self._emit_waits(e, self._deps(e, reads, writes))
        ins = fn()
        self.ninstr[e] += 1
        if inc:
            self.cnt[e] += 1
            ins.then_inc(self.semobj[e], 1)
            val = self.cnt[e]
        else:
            val = self.cnt[e] + 1
        self._record(e, val, reads, writes)
        return ins

    def dma(self, q, out, in_, **kw):
        reads = [in_]
        writes = [out]
        self._emit_waits(q, self._deps(q, reads, writes))
        owner = out.buf
        if getattr(out.buf, "is_dram", False) and not getattr(in_.buf, "is_dram", False):
            owner = in_.buf
        kind = "sw" if q == "pool" else "hw"
        if not hasattr(owner, "dsems"):
            owner.dsems = {}
            owner.dcnts_ = {}
        if kind not in owner.dsems:
            nm = "d%d%s_%s" % (self.nbuf, kind[0], owner.name)
            self.nbuf += 1
            owner.dsems[kind] = nm
            owner.dcnts_[kind] = 0
            self.semobj[nm] = self.es.enter_context(self.nc.semaphore(nm))
        nm = owner.dsems[kind]
        ins = self.eng[q].dma_start(out=out.ap, in_=in_.ap, **kw)
        self.ninstr[q] += 1
        owner.dcnts_[kind] += 16
        self.dcnts[nm] = owner.dcnts_[kind]
        ins.then_inc(self.semobj[nm], 16)
        self._record(nm, owner.dcnts_[kind], reads, writes)
        return ins

    def barrier(self):
        deps = {}
        for k in self.ENG:
            if self.cnt[k] > 0:
                deps[k] = self.cnt[k]
        deps.update(self.dcnts)
        for e in self.ENG:
            self._emit_waits(e, dict(deps))

    def mm(self, out, lhsT, rhs, start=True, stop=True, inc=None):
        if inc is None:
            inc = stop
        return self.op("pe", lambda: self.nc.tensor.matmul(_ap(out), _ap(lhsT), _ap(rhs), start=start, stop=stop),
                       reads=[lhsT, rhs], writes=[out], inc=inc)

    def tr(self, out, in_, ident, inc=True):
        return self.op("pe", lambda: self.nc.tensor.transpose(_ap(out), _ap(in_), _ap(ident)),
                       reads=[in_, ident], writes=[out], inc=inc)

    def act(self, out, in_, func, scale=1.0, bias=None, accum_out=None):
        reads = [in_, scale, bias]
        writes = [out, accum_out]
        kw = {}
        if bias is not None:
            kw["bias"] = _ap(bias)
        if accum_out is not None:
            kw["accum_out"] = _ap(accum_out)
        return self.op("act", lambda: self.nc.scalar.activation(out=_ap(out), in_=_ap(in_), func=func,
                                                                scale=_ap(scale), **kw),
                       reads=reads, writes=writes)

    def ts(self, e, out, in0, s1, op0, s2=None, op1=None):
        kw = {}
        if op1 is not None:
            kw["op1"] = op1
        return self.op(e, lambda: self.eng[e].tensor_scalar(out=_ap(out), in0=_ap(in0), scalar1=_ap(s1),
                                                            scalar2=_ap(s2), op0=op0, **kw),
                       reads=[in0, s1, s2], writes=[out])

    def tt(self, e, out, in0, in1, op):
        return self.op(e, lambda: self.eng[e].tensor_tensor(out=_ap(out), in0=_ap(in0), in1=_ap(in1), op=op),
                       reads=[in0, in1], writes=[out])

    def stt(self, out, in0, scalar, in1, op0, op1):
        return self.op("dve", lambda: self.nc.vector.scalar_tensor_tensor(out=_ap(out), in0=_ap(in0),
                                                                        scalar=_ap(scalar), in1=_ap(in1),
                                                                        op0=op0, op1=op1),
                       reads=[in0, in1, scalar], writes=[out])

    def copy(self, e, out, in_):
        if e == "act":
            return self.op("act", lambda: self.nc.scalar.copy(out=_ap(out), in_=_ap(in_)), reads=[in_], writes=[out])
        return self.op(e, lambda: self.eng[e].tensor_copy(out=_ap(out), in_=_ap(in_)), reads=[in_], writes=[out])

    def memset(self, e, out, val):
        return self.op(e, lambda: self.eng[e].memset(_ap(out), val), reads=[], writes=[out])


IN_SPECS = [
    ("x", [TOK, 1024], F32), ("xs", [NS, 1024], F32), ("sgla", [NS, 4, 128, 256], F32),
    ("ssc", [NS, 2, 1024], F32), ("scc", [NS, 30, 1024], F32),
    ("w_in_a", [1024, 7184], F32), ("w_out_a", [2048, 1024], F32),
    ("w_in_c", [1024, 3072], F32), ("w_out_c", [1024, 1024], F32),
    ("wgu", [17, 512], F32), ("g0T", [128, 8], F32), ("g1T", [128, 8], F32), ("glaT", [128, 8], F32),
    ("wsT", [128, 8, 3], F32), ("wdT", [128, 8, 31], F32), ("bdw", [128, 8], F32),
    ("lng", [128, 8], F32), ("lnb", [128, 8], F32), ("binc", [128, 24], F32),
    ("bout", [1, 1024], F32), ("bincrow", [1, 3072], F32), ("fgbc", [128, 1024], F32),
    ("c_identb", [128, 128], BF16), ("c_identf", [128, 128], F32), ("c_trif", [128, 128], F32),
    ("c_delta", [128, 16, 16], F32),
]
OUT_SPECS = [
    ("y", [TOK, 1024], F32), ("ys", [NS, 1024], F32), ("glap", [4, 128, 256], F32),
    ("scp", [2, 1024], F32), ("ccp", [30, 1024], F32), ("glas", [NS, 4, 128, 256], F32),
    ("scs", [NS, 2, 1024], F32), ("ccs", [NS, 30, 1024], F32),
]


def build_nc(stop_after=None, debug_resid=False):
    nc = bass.Bass("TRN2", target_bir_lowering=False)
    D = {}
    with ExitStack() as es:
        P = Prog(nc, es)
        for name, shape, dt in IN_SPECS:
            D[name] = P.dram(name, nc.dram_tensor(name, shape, dt, kind="ExternalInput").ap())
        for name, shape, dt in OUT_SPECS:
            D[name] = P.dram(name, nc.dram_tensor(name, shape, dt, kind="ExternalOutput").ap())
        if debug_resid:
            D["dbg"] = P.dram("dbg", nc.dram_tensor("dbg", [17, 128, 1024], F32, kind="ExternalOutput").ap())
        D["dgd"] = P.dram("dgd", nc.dram_tensor("dgd", [8, 128, 31 * 128], BF16, kind="Internal").ap())
        wscd_t = nc.dram_tensor("wscd", [8, 128, 4096], BF16, kind="Internal").ap()
        wicd_t = nc.dram_tensor("wicd", [8, 128, 3072], BF16, kind="Internal").ap()
        wscd_b = P.dram("wscd", wscd_t)
        wicd_b = P.dram("wicd", wicd_t)
        WSCD = [wscd_b[c] for c in range(8)]
        WICD = [wicd_b[c] for c in range(8)]
        WOBD = P.dram("wobd", nc.dram_tensor("wobd", [128, 8192], BF16, kind="Internal").ap())
        WOCD = P.dram("wocd", nc.dram_tensor("wocd", [128, 8192], BF16, kind="Internal").ap())

        def precast_B():
            wsrc = D["w_in_a"][:, 3088:7184].re("(kc p) (g c n) -> p kc g c n", p=128, g=4, c=8)
            for c in range(8):
                dst = WSCD[c].re("p (kc g n) -> p kc g n", kc=8, g=4)
                for g in range(4):
                    P.dma("pool", dst[:, :, g, :], wsrc[:, :, g, c, :])
            dst = WOBD.all().re("p (kc n) -> p kc n", kc=8)
            for h in range(2):
                P.dma("pool", dst[:, :, h * 512:(h + 1) * 512],
                      D["w_out_a"][1024:2048, h * 512:(h + 1) * 512].re("(kc p) n -> p kc n", p=128))

        def precast_C():
            wsrc = D["w_in_c"].all().re("(kc p) (g c n) -> p kc g c n", p=128, g=3, c=8)
            for c in range(8):
                dst = WICD[c].re("p (kc g n) -> p kc g n", kc=8, g=3)
                for g in range(3):
                    P.dma("pool", dst[:, :, g, :], wsrc[:, :, g, c, :])
            dst = WOCD.all().re("p (kc n) -> p kc n", kc=8)
            for h in range(2):
                P.dma("pool", dst[:, :, h * 512:(h + 1) * 512],
                      D["w_out_c"][:, h * 512:(h + 1) * 512].re("(kc p) n -> p kc n", p=128))

        resid = [P.sbuf("resid%d" % i, [128, 1024], F32) for i in range(17)]
        identb = P.sbuf("identb", [128, 128], BF16)
        identf = P.sbuf("identf", [128, 128], F32)
        trif = P.sbuf("trif", [128, 128], F32)
        trib = P.sbuf("trib", [128, 128], BF16)
        smalls = {}
        for nm, shp in [("g0T", [128, 8]), ("g1T", [128, 8]), ("glaT", [128, 8]), ("wsT", [128, 8, 3]),
                        ("wdT", [128, 8, 31]), ("bdw", [128, 8]), ("lng", [128, 8]), ("lnb", [128, 8]),
                        ("binc", [128, 24])]:
            smalls[nm] = P.sbuf("c_" + nm, shp, F32)
            P.dma("sp", smalls[nm].all(), D[nm].all())
        ssq1 = P.sbuf("ssq1", [128, 17], F32)
        ssq1h = P.sbuf("ssq1h", [128, 17, 2], F32)
        rstd1 = P.sbuf("rstd1", [128, 17], F32)
        onesb = P.sbuf("onesb", [128, 128], BF16)
        onesdiv = P.sbuf("onesdiv", [128, 128], BF16)
        P.dma("sp", identb.all(), D["c_identb"].all())
        P.dma("sp", identf.all(), D["c_identf"].all())
        P.dma("sp", trif.all(), D["c_trif"].all())
        P.copy("dve", trib.all(), trif.all())
        trib16 = P.sbuf("trib16", [128, 128], BF16)
        neg16 = P.sbuf("neg16", [128, 1], BF16)
        P.ts("dve", trib16.all(), trif.all(), -1.0 / 16, ALU.mult)
        P.memset("dve", neg16.all(), -1.0 / 16)
        P.memset("dve", onesb.all(), 1.0)
        P.memset("dve", onesdiv.all(), 1.0 / 1024)
        P.memset("dve", ssq1h.all(), 1.0)

        banks = [P.psum("ps%d" % i, [128, 512], F32) for i in range(8)]
        ring = Ring(banks)

        def load_w(dst, src_view, ncols, q="pool"):
            c = 0
            while c < ncols:
                w = min(1024, ncols - c)
                P.dma(q, dst[:, :, c:c + w], src_view[:, c:c + w].re("(kc p) n -> p kc n", p=128))
                c += w

        def scale_rows(dst, gT, ncols, e="dve"):
            for kc in range(8):
                P.ts(e, dst[:, kc, 0:ncols], dst[:, kc, 0:ncols], gT[:, kc:kc + 1], ALU.mult)

        def rstd_from_ssq(dst, src, n, eps):
            P.act(dst, src, AF.Ln, scale=1.0 / n, bias=eps_t[eps][: dst.ap.shape[0], 0:1])
            P.act(dst, dst, AF.Exp, scale=-0.5)

        eps_t = {}
        for ev in (RMS_EPS, LN_EPS, 1.0):
            eps_t[ev] = P.sbuf("eps%g" % ev, [128, 1], F32)
            P.memset("dve", eps_t[ev].all(), float(ev))

        def transposes_to(dst_view_fn, src, ntok, nchunks, ident, dt_bf=True, e="act", gT=None):
            pb = ring.get()
            pv = pb.all().cast(BF16) if dt_bf else pb.all()
            for kc in range(nchunks):
                P.tr(pv[:, kc * ntok:(kc + 1) * ntok], src[:ntok, kc * 128:(kc + 1) * 128], ident[:ntok, :ntok],
                     inc=(kc == nchunks - 1))
            if gT is None:
                P.copy(e, dst_view_fn(), pv[:, 0:nchunks * ntok].re("p (k t) -> p k t", k=nchunks))
            else:
                P.tt("dve", dst_view_fn(), pv[:, 0:nchunks * ntok].re("p (k t) -> p k t", k=nchunks),
                     gT.all().re("p (k o) -> p k o", o=1).bc([128, nchunks, ntok]), ALU.mult)

        es_hT = ExitStack()
        hT = P.sbuf("hT", [128, 8, TOK + NS], BF16, es_hT)
        with ExitStack() as esA:
            WG = {}
            for nm, c0, w in (("a", 3072, 16), ("q", 0, 512), ("k", 512, 512), ("v0", 1024, 512), ("v1", 1536, 512),
                              ("g0", 2048, 512), ("g1", 2560, 512)):
                WG[nm] = (P.sbuf("Wg_" + nm, [128, 8, w], BF16, esA), c0, w)
            Wot = [P.sbuf("Wot%d" % n, [128, 8, 512], BF16, esA) for n in range(2)]
            wgu = P.sbuf("wgu", [32, 512], BF16, esA)
            delta = P.sbuf("delta", [128, 16, 16], F32, esA)
            P.dma("sp", delta.all(), D["c_delta"].all())
            P.dma("pool", wgu[0:17, :], D["wgu"].all())
            for nm in ("a", "q", "k", "v0", "v1", "g0", "g1"):
                wb, c0, w = WG[nm]
                load_w(wb, D["w_in_a"][:, c0:c0 + w], w)
            for n in range(2):
                load_w(Wot[n], D["w_out_a"][0:1024, n * 512:(n + 1) * 512], 512)
                scale_rows(Wot[n], smalls["glaT"], 512)

            ssqs = [P.sbuf("ssq%d" % i, [128, 1], F32, esA) for i in range(2)]
            rstds = [P.sbuf("rstd%d" % i, [128, 1], F32, esA) for i in range(2)]
            om = P.sbuf("om", [128, 1024], BF16, esA)
            omTf = P.sbuf("omTf", [128, 1024], BF16, esA)
            omT = omTf.all().re("p (k t) -> p k t", k=8)
            xns = [om, omTf]
            aaug = P.sbuf("aaug", [32, 128], BF16, esA)
            eb = P.sbuf("eb", [128, 512], F32, esA)
            qe = P.sbuf("qe", [128, 512], BF16, esA)
            sgh = P.sbuf("sgh", [128, 512], F32, esA)
            ossq = P.sbuf("ossq", [128, 4], F32, esA)
            rso = P.sbuf("rso", [128, 4], F32, esA)
            atok = eb
            qtok = qe
            SB = {"aT": P.sbuf("aT", [128, 4, 16], F32, esA), "qT": P.sbuf("qT", [128, 4, 16], BF16, esA),
                  "Qm": P.sbuf("Qm", [128, 4, 16, 16], BF16, esA), "vbfS": P.sbuf("vbfS", [128, 1024], BF16, esA),
                  "ksb": P.sbuf("ksb", [16, 512], F32, esA)}
            esAp = ExitStack()
            la = P.sbuf("la", [128, 512], BF16, esAp)
            enb = P.sbuf("enb", [128, 512], F32, esAp)
            scm = P.sbuf("scm", [128, 4, 128], BF16, esAp)
            S = P.sbuf("S", [128, 4, 256], F32, esAp)
            Sbf = P.sbuf("Sbf", [128, 4, 256], BF16, esAp)
            FBs = [{"qkT": P.sbuf("qkT%d" % i, [128, 8, 128], BF16, esAp), "ke": P.sbuf("ke%d" % i, [128, 512], BF16, esAp),
                    "vbf": P.sbuf("vbf%d" % i, [128, 1024], BF16, esAp), "dec": P.sbuf("dec%d" % i, [128, 4], F32, esAp)}
                   for i in range(2)]

            P.memset("dve", aaug.all(), 1.0)
            P.memset("dve", S.all(), 0.0)

            def tinfo(t):
                sample = (t == 16)
                return sample, (NS if sample else 128), (TOK if sample else t * 128)

            def proj_tok(ps, col0, ntok, nm):
                wb, _, ncols = WG[nm]
                for kc in range(8):
                    P.mm(ps[:ntok, :ncols], hT[:, kc, col0:col0 + ntok], wb[:, kc, :],
                         start=(kc == 0), stop=(kc == 7))

            for t in range(17):
                if t == 16:
                    P.dma("sp", resid[16][:NS, :], D["xs"].all())
                else:
                    P.dma("sp", resid[t].all(), D["x"][t * 128:(t + 1) * 128, :])
            for t in range(17):
                sample, ntok, col0 = tinfo(t)
                xt, xn, ssq, rstd = resid[t], xns[t % 2], ssqs[t % 2], rstds[t % 2]
                P.act(xn[:ntok, :], xt[:ntok, :], AF.Square, accum_out=ssq[:ntok, :])
                rstd_from_ssq(rstd[:ntok, :], ssq[:ntok, :], 1024, RMS_EPS)
                P.ts("dve", xn[:ntok, :], xt[:ntok, :], rstd[:ntok, :], ALU.mult)
                transposes_to(lambda: hT[:, :, col0:col0 + ntok], xn, ntok, 8, identb, e="mix", gT=smalls["g0T"])

            FR = {}

            def front(t):
                sample, ntok, col0 = tinfo(t)
                fb = FBs[t % 2]
                qkT, ke, vbf, dec = fb["qkT"], fb["ke"], fb["vbf"], fb["dec"]
                pa = ring.get()
                for kc in range(8):
                    P.mm(pa[:16, :ntok], WG["a"][0][:, kc, :], hT[:, kc, col0:col0 + ntok],
                         start=(kc == 0), stop=(kc == 7))
                P.copy("dve", aaug[0:16, :ntok], pa[:16, :ntok])
                yield 1.2
                pla = ring.get()
                P.mm(pla[:ntok, :], aaug[0:17, :ntok], wgu[0:17, :])
                pe1 = ring.get()
                P.act(pe1[:ntok, :], pla[:ntok, :], AF.Exp, scale=-1.0)
                P.act(la.all(), pe1.all(), AF.Ln, bias=eps_t[1.0][:, 0:1])
                yield 2.0
                pb = ring.get()
                P.mm(pb.all(), trib16.all(), la.all())
                pbl = ring.get()
                for h in range(4):
                    P.mm(pbl[:, h:h + 1], la[:, h * 128:(h + 1) * 128], neg16[:, 0:1], inc=(h == 3))
                P.act(eb.all(), pb.all(), AF.Exp)
                P.act(enb.all(), pb.all(), AF.Exp, scale=-1.0)
                P.act(dec.all(), pbl[:, 0:4], AF.Exp)
                yield 2.2
                pq = ring.get()
                proj_tok(pq, col0, ntok, "q")
                P.stt(qe.all(), pq.all(), 128 ** -0.5, eb.all(), ALU.mult, ALU.mult)
                yield 2.6
                pk = ring.get()
                proj_tok(pk, col0, ntok, "k")
                P.tt("dve", ke.all(), pk.all(), enb.all(), ALU.mult)
                yield 2.6
                tq = ring.get()
                tqv = tq.all().cast(BF16)
                for h in range(4):
                    P.tr(tqv[:, h * 128:(h + 1) * 128], qe[:, h * 128:(h + 1) * 128], identb.all(), inc=False)
                for h in range(4):
                    P.tr(tqv[:, 512 + h * 128:512 + (h + 1) * 128], ke[:, h * 128:(h + 1) * 128], identb.all(),
                         inc=(h == 3))
                P.copy("dve", qkT.all().re("p k t -> p (k t)"), tqv)
                yield 1.8
                for n in range(2):
                    pv = ring.get()
                    proj_tok(pv, col0, ntok, "v%d" % n)
                    P.copy("act", vbf[:ntok, n * 512:(n + 1) * 512], pv[:ntok, :])
                    yield 2.5

            def front_sample():
                t = 16
                sample, ntok, col0 = tinfo(t)
                vbf = SB["vbfS"]
                aT, qT, Qm = SB["aT"], SB["qT"], SB["Qm"]
                pa = ring.get()
                for kc in range(8):
                    P.mm(pa[:16, :ntok], WG["a"][0][:, kc, :], hT[:, kc, col0:col0 + ntok],
                         start=(kc == 0), stop=(kc == 7))
                P.copy("dve", aaug[0:16, :ntok], pa[:16, :ntok])
                pla = ring.get()
                P.mm(pla[:ntok, :], aaug[0:17, :ntok], wgu[0:17, :])
                pe1 = ring.get()
                P.act(pe1[:ntok, :], pla[:ntok, :], AF.Exp, scale=-1.0)
                pl1 = ring.get()
                P.act(pl1[:ntok, :], pe1[:ntok, :], AF.Ln, bias=eps_t[1.0][:ntok, 0:1])
                P.act(atok[:NS, :], pl1[:NS, :], AF.Exp, scale=-1.0 / 16)
                pq = ring.get()
                proj_tok(pq, col0, ntok, "q")
                P.ts("dve", qtok[:NS, :], pq[:NS, :], 128 ** -0.5, ALU.mult)
                yield 4.0
                pk = ring.get()
                proj_tok(pk, col0, ntok, "k")
                P.copy("dve", SB["ksb"].all(), pk[:NS, :])
                yield 2.5
                for n in range(2):
                    pv = ring.get()
                    proj_tok(pv, col0, ntok, "v%d" % n)
                    P.copy("act", vbf[:ntok, n * 512:(n + 1) * 512], pv[:ntok, :])
                    yield 2.5
                pt = ring.get()
                for h in range(4):
                    P.tr(pt[:, h * 16:(h + 1) * 16], atok[:NS, h * 128:(h + 1) * 128], identf[:16, :16], inc=(h == 3))
                P.copy("act", aT.all().re("p h s -> p (h s)"), pt[:, 0:64])
                pt2 = ring.get()
                pt2v = pt2.all().cast(BF16)
                for h in range(4):
                    P.tr(pt2v[:, h * 16:(h + 1) * 16], qtok[:NS, h * 128:(h + 1) * 128], identb[:16, :16], inc=(h == 3))
                P.copy("act", qT.all().re("p h s -> p (h s)"), pt2v[:, 0:64])
                yield 2.0

            def core(t):
                fb = FBs[t % 2]
                qkT, ke, vbf, dec = fb["qkT"], fb["ke"], fb["vbf"], fb["dec"]
                psc = ring.get()
                for h in range(4):
                    P.mm(psc[:, h * 128:(h + 1) * 128], qkT[:, 4 + h, :], qkT[:, h, :], inc=(h == 3))
                P.tt("dve", scm.all(), psc.all().re("p (h l) -> p h l", h=4),
                     trif.all().re("p (o l) -> p o l", o=1).bc([128, 4, 128]), ALU.mult)
                yield 1.5
                po = [ring.get(hold=True), ring.get(hold=True)]
                pos = []
                for h in range(4):
                    ov = po[h // 2][:, (h % 2) * 256:(h % 2 + 1) * 256]
                    P.mm(ov, scm[:, h, :], vbf[:, h * 256:(h + 1) * 256], start=True, stop=(t == 0))
                    if t > 0:
                        P.mm(ov, qkT[:, h, :], Sbf[:, h, :], start=False, stop=True)
                    pos.append(ov)
                FR["pos"] = pos
                yield 1.2
                for h2 in range(2):
                    pd = ring.get()
                    for hh in range(2):
                        h = h2 * 2 + hh
                        P.mm(pd[:, hh * 256:(hh + 1) * 256], ke[:, h * 128:(h + 1) * 128],
                             vbf[:, h * 256:(h + 1) * 256], inc=(hh == 1))
                    for hh in range(2):
                        h = h2 * 2 + hh
                        P.tt("dve", S[:, h, :], pd[:, hh * 256:(hh + 1) * 256], S[:, h, :], ALU.add)
                        P.act(S[:, h, :], S[:, h, :], AF.Copy, scale=dec[:, h:h + 1])
                        if t < 15:
                            P.copy("dve", Sbf[:, h, :], S[:, h, :])
                    yield 1.5
                if t == 15:
                    P.dma("sp", D["glap"].all().re("h d v -> d h v"), S.all())

            def core_sample():
                pk = SB["ksb"]
                vbf = SB["vbfS"]
                aT, qT, Qm, kms, s0b, snbf = SB["aT"], SB["qT"], SB["Qm"], SB["kms"], SB["s0b"], SB["snbf"]
                qaT, qkp, qk, osb = SB["qaT"], SB["qkp"], SB["qk"], SB["osb"]
                P.tt("dve", qaT.all(), qT.all(), aT.all(), ALU.mult)
                for h in range(4):
                    P.tt("dve", Qm[:, h, :, :], qaT[:, h, :].re("p (s o) -> p s o", o=1).bc([128, 16, 16]),
                         delta.all(), ALU.mult)
                P.tt("dve", qkp, qtok[:NS, :], pk[:NS, :], ALU.mult)
                P.op("dve", lambda: nc.vector.tensor_reduce(out=qk.all().ap, in_=qkp.re("s (h d) -> s h d", h=4).ap,
                                                            axis=mybir.AxisListType.X, op=ALU.add),
                     reads=[qkp], writes=[qk.all()])
                def load_s0(si):
                    P.dma("sp", s0b[si % 4].all(), D["sgla"][si].re("h d v -> d h v"))

                for si in range(3):
                    load_s0(si)
                pobs = [ring.get(hold=True) for _ in range(4)]
                kms2 = SB["kms2"]
                stg = {}

                def stage1(si):
                    km = kms2[0]
                    P.ts("dve", km.all(), pk[:NS, :], identf[:NS, si:si + 1], ALU.mult)
                    pds = [ring.get(), ring.get()]
                    for h in range(4):
                        P.mm(pds[h // 2][:, (h % 2) * 256:(h % 2 + 1) * 256], km[:, h * 128:(h + 1) * 128],
                             vbf[:NS, h * 256:(h + 1) * 256])
                    stg[si] = pds

                stage1(0)
                for si in range(NS):
                    if si + 3 < NS:
                        load_s0(si + 3)
                    if si + 1 < NS:
                        stage1(si + 1)
                    pds = stg.pop(si)
                    s0 = s0b[si % 4]
                    sn = s0
                    s0f = snbf[si % 2]
                    P.copy("act", s0f.all(), s0.all())
                    for h in range(4):
                        P.stt(sn[:, h, :], s0[:, h, :], aT[:, h, si:si + 1],
                              pds[h // 2][:, (h % 2) * 256:(h % 2 + 1) * 256], ALU.mult, ALU.add)
                        P.mm(pobs[h][:NS, 0:256], Qm[:, h, si, :], s0f[:, h, :], start=(si == 0), stop=(si == NS - 1))
                    P.dma("pool", D["glas"][si].re("h d v -> d h v"), sn.all())
                pos = []
                for h in range(4):
                    P.stt(osb[:, h * 256:(h + 1) * 256], vbf[:NS, h * 256:(h + 1) * 256], qk[:, h:h + 1],
                          pobs[h][:NS, 0:256], ALU.mult, ALU.add)
                    ring.release(pobs[h])
                    pos.append(osb[:, h * 256:(h + 1) * 256])
                return pos

            def tail(t, pos):
                sample, ntok, col0 = tinfo(t)
                xt = resid[t]
                pj = ring.get()
                for h in range(4):
                    P.act(pj[:ntok, 0:256], pos[h][:ntok, :], AF.Square, accum_out=ossq[:ntok, h:h + 1])
                rstd_from_ssq(rso[:ntok, :], ossq[:ntok, :], 256, RMS_EPS)
                for n in range(2):
                    pg = ring.get()
                    proj_tok(pg, col0, ntok, "g%d" % n)
                    P.act(sgh[:ntok, :], pg[:ntok, :], AF.Silu)
                    for hh in range(2):
                        h = n * 2 + hh
                        P.stt(om[:ntok, h * 256:(h + 1) * 256], pos[h][:ntok, :], rso[:ntok, h:h + 1],
                              sgh[:ntok, hh * 256:(hh + 1) * 256], ALU.mult, ALU.mult)
                    yield 3.0
                for b_ in set(id(v.buf) for v in pos):
                    ring.held.discard(b_)
                transposes_to(lambda: omT[:, :, :ntok], om, ntok, 8, identb, e="dve")
                yield 1.5
                for n in range(2):
                    pout = ring.get()
                    for kc in range(8):
                        P.mm(pout[:ntok, :], omT[:, kc, :ntok], Wot[n][:, kc, :],
                             start=(kc == 0), stop=(kc == 7))
                    P.tt("dve", xt[:ntok, n * 512:(n + 1) * 512], pout[:ntok, :],
                         xt[:ntok, n * 512:(n + 1) * 512], ALU.add)
                    yield 2.5

            def merge(*gens, skew=()):
                gens = [g for g in gens if g is not None]
                clk = [0.0] * len(gens)
                for i, v in enumerate(skew):
                    if i < len(clk):
                        clk[i] = v
                alive = list(range(len(gens)))
                while alive:
                    i = min(alive, key=lambda k: clk[k])
                    try:
                        dt = next(gens[i])
                        clk[i] += dt if dt else 1.0
                    except StopIteration:
                        alive.remove(i)

            def body(t):
                yield from core(t)
                yield from tail(t, FR["pos"])

            merge(front(0))
            for t in range(16):
                if t == 1:
                    precast_B()
                if t == 5:
                    precast_C()
                merge(front(t + 1) if t + 1 < 16 else front_sample(), body(t))
            P.barrier()
            esAp.close()
            with ExitStack() as esAs:
                SB.update({"kms": None, "kms2": [P.sbuf("kms2%d" % i, [16, 512], BF16, esAs) for i in range(1)],
                           "s0b": [P.sbuf("s0b%d" % i, [128, 4, 256], F32, esAs) for i in range(4)],
                           "snbf": [P.sbuf("snbf%d" % i, [128, 4, 256], BF16, esAs) for i in range(2)],
                           "qaT": P.sbuf("qaT", [128, 4, 16], BF16, esAs), "qkp": sgh[:NS, :],
                           "qk": P.sbuf("qk", [16, 4], F32, esAs)})
                SB["osb"] = SB["s0b"][0].all().re("p h v -> p (h v)")[:NS, :]
                pos = core_sample()
                merge(tail(16, pos))
                P.barrier()

        if stop_after == "A":
            r_ = _finish(nc, P, D, resid, debug_resid)
            es_hT.close()
            return r_

        NB = 256
        with ExitStack() as esB:
            WscB = [P.sbuf("Wsc%d" % c, [128, 8, 4, 128], BF16, esB) for c in range(8)]
            Wob = P.sbuf("Wob", [128, 8, 1024], BF16, esB)
            for c in range(8):
                P.dma("sp", WscB[c].all().re("p kc g n -> p (kc g n)"), WSCD[c])
            P.dma("sp", Wob.all().re("p kc n -> p (kc n)"), WOBD.all())
            dg3 = P.sbuf("dg3", [128, 8, 3, 128], BF16, esB)
            for c in range(8):
                for j in range(3):
                    P.ts("dve", dg3[:, c, j, :], identb.all(), smalls["wsT"][:, c, j:j + 1], ALU.mult)
            hbs = [P.sbuf("hbs%d" % i, [128, NB], F32, esB) for i in range(2)]
            szb = [P.sbuf("szb%d" % i, [128, NB], F32, esB) for i in range(2)]
            yT = P.sbuf("yT", [128, 8, NB], BF16, esB)

            def phaseB_tile(T, sample, B, hook=None):
                N = NS if sample else NB
                col0 = TOK if sample else T * NB
                last = (not sample) and (T == TOK // NB - 1)
                if sample:
                    sscb, ue, usf, usT = B["sscb"], B["ue"], B["usf"], B["usT"]
                    P.dma("act", D["scs"][:, 0, :], D["ssc"][:, 1, :])
                    pt = ring.get()
                    for j in range(2):
                        P.dma("act", sscb.all(), D["ssc"][:, j, :])
                        for c in range(8):
                            jc = j * 8 + c
                            P.tr(pt[:, jc * 16:(jc + 1) * 16], sscb[:, c * 128:(c + 1) * 128], identf[:16, :16],
                                 inc=True)
                    for j in range(2):
                        P.copy("act", ue[:, :, j, :], pt[:, j * 128:(j + 1) * 128].re("p (c s) -> p c s", c=8))
                else:
                    ub, ust, ustT = B["ub"], B["ust"], B["ustT"]
                for c in range(8):
                    ps = []
                    for g in range(4):
                        pb_ = ring.get()
                        for kc in range(8):
                            P.mm(pb_[:, :N], WscB[c][:, kc, g, :], hT[:, kc, col0:col0 + N],
                                 start=(kc == 0), stop=(kc == 7))
                        ps.append(pb_)
                    phb, pgb, pgc, pzb = ps
                    hb = hbs[c % 2]
                    sz = szb[c % 2]
                    P.copy("act", hb[:, :N], phb[:, :N])
                    py = ring.get()
                    if not sample:
                        if T > 0:
                            P.copy("dve", ub[c][:, 0:2], ub[c][:, NB:NB + 2])
                        P.tt("dve", ub[c][:, 2:NB + 2], pgc[:, :N], hb[:, :N], ALU.mult)
                        if last:
                            P.tt("dve", ust[:, :, c], pgc[:, NB - 2:NB], hb[:, NB - 2:NB], ALU.mult)
                        for j in range(3):
                            P.mm(py[:, :N], dg3[:, c, j, :], ub[c][:, j:j + NB], start=(j == 0), stop=(j == 2))
                    else:
                        P.tt("dve", usf[:, c, :], pgc[:, :N], hb[:, :N], ALU.mult)
                        P.copy("dve", ue[:, c, 2, :], usf[:, c, :])
                        for j in range(3):
                            P.mm(py[:, :N], dg3[:, c, j, :], ue[:, c, j, :], start=(j == 0), stop=(j == 2))
                    P.act(sz[:, :N], pzb[:, :N], AF.Silu)
                    P.tt("dve", sz[:, :N], pgb[:, :N], sz[:, :N], ALU.mult)
                    P.tt("dve", yT[:, c, :N], py[:, :N], sz[:, :N], ALU.mult)
                    if hook is not None:
                        hook(c)
                nsub = 1 if sample else NB // 128
                for s in range(nsub):
                    ntok = NS if sample else 128
                    ti = 16 if sample else T * nsub + s
                    for n in range(2):
                        pout = ring.get()
                        for c in range(8):
                            P.mm(pout[:ntok, :], yT[:, c, s * 128:s * 128 + ntok], Wob[:, c, n * 512:(n + 1) * 512],
                                 start=(c == 0), stop=(c == 7))
                        P.tt("dve", resid[ti][:ntok, n * 512:(n + 1) * 512], pout[:ntok, :],
                             resid[ti][:ntok, n * 512:(n + 1) * 512], ALU.add)
                    for n in range(2):
                        P.act(ring.get()[:ntok, :], resid[ti][:ntok, n * 512:(n + 1) * 512], AF.Square,
                              accum_out=ssq1h[:ntok, ti, n:n + 1])
                if last:
                    pt = ring.get()
                    P.tr(pt[:16, 0:128], ust.all().re("p j c -> p (j c)"), identf.all())
                    P.copy("act", ustT.all(), pt[:16, 0:128])
                    P.dma("sp", D["scp"].all().re("j (c p) -> (j c) p", p=128), ustT.all())
                if sample:
                    for half in range(2):
                        pt = ring.get()
                        for cc in range(4):
                            c = half * 4 + cc
                            P.tr(pt[:16, cc * 128:(cc + 1) * 128], usf[:, c, :], identf.all(), inc=(cc == 3))
                        P.copy("act", usT[:, half * 512:(half + 1) * 512], pt[:16, :])
                    P.dma("sp", D["scs"][:, 1, :], usT.all())

            with ExitStack() as esBs:
                B = {"sscb": P.sbuf("sscb", [16, 1024], F32, esBs), "ue": P.sbuf("ue", [128, 8, 3, 16], BF16, esBs),
                     "usf": P.sbuf("usf", [128, 8, 16], F32, esBs)}
                B["usT"] = B["sscb"]
                phaseB_tile(0, True, B)
                P.barrier()
            with ExitStack() as esBp:
                B = {"ub": [P.sbuf("ub%d" % c, [128, NB + 2], BF16, esBp) for c in range(8)],
                     "ust": P.sbuf("ust", [128, 2, 8], F32, esBp), "ustT": P.sbuf("ustT", [16, 128], F32, esBp)}
                for c in range(8):
                    P.memset("dve", B["ub"][c][:, 0:2], 0.0)
                dgb = P.sbuf("dgb", [128, 16, 128], BF16, esBp)
                def dg_hook(T):
                    def hook(c):
                        cc = T
                        dsrc = D["dgd"][cc].re("p (j q) -> p j q", j=31)
                        j0, j1 = c * 4, min(31, c * 4 + 4)
                        base = 0 if c < 4 else 16
                        for j in range(j0, j1):
                            P.ts("dve", dgb[:, j - base, :], identb.all(), smalls["wdT"][:, cc, j:j + 1], ALU.mult)
                        if c == 3:
                            P.dma("sp", dsrc[:, 0:16, :], dgb[:, 0:16, :])
                        if c == 7:
                            P.dma("sp", dsrc[:, 16:31, :], dgb[:, 0:15, :])
                    return hook

                for T in range(TOK // NB):
                    phaseB_tile(T, False, B, hook=dg_hook(T))
                P.barrier()

        if stop_after == "B":
            r_ = _finish(nc, P, D, resid, debug_resid)
            es_hT.close()
            return r_
        es_hT.close()

        with ExitStack() as esC:
            WicB = [P.sbuf("Wic%d" % c, [128, 8, 3, 128], BF16, esC) for c in range(8)]
            Woc = P.sbuf("Woc", [128, 8, 1024], BF16, esC)
            brows = P.sbuf("brows", [64, 1024], BF16, esC)
            boutb = brows
            P.dma("pool", brows[0:1, :], D["bout"].all())
            bh = P.sbuf("bh", [128, 24], F32, esC)
            P.ts("dve", bh.all(), smalls["binc"].all(), 0.5, ALU.mult)
            P.dma("pool", brows[32:33, :], D["bincrow"][:, 0:1024])
            P.ts("dve", brows[32:33, :], brows[32:33, :], 0.5, ALU.mult)
            for c in range(8):
                P.dma("sp", WicB[c].all().re("p kc g n -> p (kc g n)"), WICD[c])
                P.ts("dve", WicB[c][:, :, 0, :], WicB[c][:, :, 0, :], 0.5, ALU.mult)
            P.dma("sp", Woc.all().re("p kc n -> p (kc n)"), WOCD.all())
            fgbc = P.sbuf("fgbc", [128, 1024], F32, esC)
            P.dma("sp", fgbc.all(), D["fgbc"].all())
            NT = 256
            dgr = [P.sbuf("dgr%d" % i, [128, 16, 128], BF16, esC) for i in range(4)]
            dg_built = [True]
            P.tt("dve", ssq1.all(), ssq1h[:, :, 0], ssq1h[:, :, 1], ALU.add)
            rstd_from_ssq(rstd1.all(), ssq1.all(), 1024, RMS_EPS)
            ufl = P.sbuf("ufl", [128, 8, 30], F32, esC)
            ssq2 = P.sbuf("ssq2", [128, 1], F32, esC)
            ssq2h = P.sbuf("ssq2h", [128, 2], F32, esC)
            rstd2 = P.sbuf("rstd2", [128, 1], F32, esC)
            xnC = P.sbuf("xnC", [128, 1024], BF16, esC)
            dgi = [0]

            FRC = {}

            def cinfo(Tc, sample):
                N = NS if sample else NT
                nsub = 1 if sample else NT // 128
                ntok = NS if sample else 128
                last = (not sample) and (Tc == TOK // NT - 1)
                return N, nsub, ntok, last

            def frontC(Tc, sample, C):
                N, nsub, ntok, last = cinfo(Tc, sample)
                b = Tc % 2
                h1T, yc = C["h1T"][b], C["yc"][b]
                sgb, yst = C["sgb"], C["yst"]
                for s in range(nsub):
                    ti = 16 if sample else Tc * nsub + s
                    P.ts("dve", xnC[:ntok, :], resid[ti][:ntok, :], rstd1[:ntok, ti:ti + 1], ALU.mult)
                    transposes_to(lambda: h1T[:, :, s * 128:s * 128 + ntok], xnC, ntok, 8, identb, e="dve",
                                  gT=smalls["g1T"])
                    yield 2.5
                if sample:
                    sccb, bT, ucs, ufs = C["sccb"], C["bT"], C["ucs"], C["ufs"]
                    P.dma("sp", D["ccs"][:, 0:29, :], D["scc"][:, 1:30, :])
                    for q in range(4):
                        P.dma("sp", sccb.all(), D["scc"].all().re("s j c -> (s j) c")[q * 120:(q + 1) * 120, :])
                        for c in range(8):
                            pt = ring.get()
                            P.tr(pt[:, 0:120], sccb[:, c * 128:(c + 1) * 128], identf[:120, :120])
                            P.copy("act", bT[:, c, q * 120:(q + 1) * 120], pt[:, 0:120])
                else:
                    uc = C["uc"]
                pst = ring.get(hold=True)

                def proj_c(c):
                    pab = ring.get(hold=True)
                    for n0 in range(0, N, 128):
                        n1 = min(N, n0 + 128)
                        P.mm(pab[:, n0:n1], brows[32:33, c * 128:(c + 1) * 128], onesb[32:33, 0:n1 - n0],
                             start=(n0 == 0), stop=False)
                    for kc in range(8):
                        P.mm(pab[:, 0:N], WicB[c][:, kc, 0, :], h1T[:, kc, :N], start=False, stop=(kc == 7))
                    for kc in range(8):
                        P.mm(pab[:, 256:256 + N], WicB[c][:, kc, 1, :], h1T[:, kc, :N],
                             start=(kc == 0), stop=(kc == 7))
                    return pab

                nxt = proj_c(0)
                pend = []
                for c in range(8):
                    pab = nxt
                    pa, pag = pab[:, 0:N], pab[:, 256:256 + N]
                    sgc = sgb[c % 2]
                    P.act(sgc[:, :N], pag, AF.Tanh, scale=0.5, bias=bh[:, 8 + c:9 + c])
                    dgh = [dgr[(dgi[0] + k) % 4] for k in range(2)]
                    dgi[0] += 2
                    dsrc = D["dgd"][c].re("p (j q) -> p j q", j=31)
                    for k, (j0, j1) in enumerate(((0, 16), (16, 31))):
                        if not dg_built[0]:
                            for j in range(j0, j1):
                                P.ts("dve", dgh[k][:, j - j0, :], identb.all(), smalls["wdT"][:, c, j:j + 1], ALU.mult)
                            P.dma("act", dsrc[:, j0:j1, :], dgh[k][:, 0:j1 - j0, :])
                        else:
                            P.dma("sp", dgh[k][:, 0:j1 - j0, :], dsrc[:, j0:j1, :])

                    def dgt_tap(j):
                        return dgh[j // 16][:, j % 16, :]
                    if not sample:
                        P.stt(uc[c][:, 30:30 + NT], sgc[:, :N], 1.0, pa, ALU.add, ALU.mult)
                        if last:
                            P.stt(ufl[:, c, :], sgc[:, NT - 30:NT], 1.0, pab[:, NT - 30:NT], ALU.add, ALU.mult)
                    else:
                        P.stt(ufs[:, c, :], sgc[:, :N], 1.0, pa, ALU.add, ALU.mult)
                        P.copy("dve", ucs[:, c, :], ufs[:, c, :])
                    ring.release(pab)
                    if c + 1 < 8:
                        nxt = proj_c(c + 1)
                    yield 1.7
                    py = ring.get()
                    if not sample:
                        for j in range(31):
                            P.mm(py[:, :N], dgt_tap(j), uc[c][:, j:j + NT], start=(j == 0), stop=(j == 30))
                        if not last:
                            P.copy("pool", uc[c][:, 0:30], uc[c][:, NT:NT + 30])
                    else:
                        for j in range(30):
                            P.mm(py[:, :N], dgt_tap(j), bT[:, c, :].re("p (s j) -> p j s", j=30)[:, j, :],
                                 start=(j == 0), stop=False)
                        P.mm(py[:, :N], dgt_tap(30), ucs[:, c, :], start=False, stop=True)
                    yield 3.4
                    while pend:
                        pend.pop(0)()
                    ys = yst[c % 2]
                    P.act(yc[:, c, :N], py[:, :N], AF.Identity, bias=smalls["bdw"][:, c:c + 1])
                    P.act(ys[:, 1, :], py[:, :N], AF.Square, bias=smalls["bdw"][:, c:c + 1])
                    P.copy("pool", ys[:, 0, :], yc[:, c, :N])
                    pend.append(lambda ys=ys, c=c: P.mm(pst[:, 0:2 * N], onesdiv.all(), ys.all().re("p a n -> p (a n)"),
                                                        start=(c == 0), stop=(c == 7)))
                    yield 1.3
                while pend:
                    pend.pop(0)()
                dg_built[0] = True
                FRC[(Tc, sample)] = pst

            def backC(Tc, sample, C):
                N, nsub, ntok, last = cinfo(Tc, sample)
                b = Tc % 2
                h1T, yc = C["h1T"][b], C["yc"][b]
                msq, var, dd, t2, sl, szc, ycT = C["msq"], C["var"], C["dd"], C["t2"], C["sl"], C["szc"], C["ycT"]
                pst = FRC.pop((Tc, sample))
                mean, ex2 = pst[:, 0:N], pst[:, N:2 * N]
                P.act(msq[:, :N], mean, AF.Square)
                P.tt("dve", var[:, :N], ex2, msq[:, :N], ALU.subtract)
                P.act(var[:, :N], var[:, :N], AF.Ln, bias=eps_t[LN_EPS][:, 0:1])
                P.act(var[:, :N], var[:, :N], AF.Exp, scale=-0.5)
                prn = ring.get(hold=True)
                prs, nmr = prn[:, 0:N], prn[:, 256:256 + N]
                P.copy("act", prs, var[:, :N])
                P.stt(nmr, mean, -1.0, var[:, :N], ALU.mult, ALU.mult)
                ring.release(pst)
                yield 3.0
                for c in range(8):
                    P.tt("dve", dd[c % 2][:, :N], yc[:, c, :N], prs, ALU.mult)
                    P.tt("dve", t2[c % 2][:, :N], dd[c % 2][:, :N], nmr, ALU.add)
                    pz = ring.get()
                    for kc in range(8):
                        P.mm(pz[:, :N], WicB[c][:, kc, 2, :], h1T[:, kc, :N],
                             start=(kc == 0), stop=(kc == 7))
                    P.act(sl[c % 2][:, :N], t2[c % 2][:, :N], AF.Silu, scale=smalls["lng"][:, c:c + 1],
                          bias=smalls["lnb"][:, c:c + 1])
                    P.act(szc[c % 2][:, :N], pz[:, :N], AF.Silu, bias=smalls["binc"][:, 16 + c:17 + c])
                    P.tt("pool", ycT[:, c, :N], sl[c % 2][:, :N], szc[c % 2][:, :N], ALU.mult)
                    yield 2.6
                ring.release(prn)
                for s in range(nsub):
                    ti = 16 if sample else Tc * nsub + s
                    for n in range(2):
                        pout = ring.get()
                        P.mm(pout[:ntok, :], onesb[0:1, :ntok], boutb[0:1, n * 512:(n + 1) * 512], start=True, stop=False)
                        for c in range(8):
                            P.mm(pout[:ntok, :], ycT[:, c, s * 128:s * 128 + ntok], Woc[:, c, n * 512:(n + 1) * 512],
                                 start=False, stop=(c == 7))
                        P.tt("dve", resid[ti][:ntok, n * 512:(n + 1) * 512], pout[:ntok, :],
                             resid[ti][:ntok, n * 512:(n + 1) * 512], ALU.add)
                        yield 2.6
                    for n in range(2):
                        P.act(ring.get()[:ntok, :], resid[ti][:ntok, n * 512:(n + 1) * 512], AF.Square,
                              accum_out=ssq2h[:ntok, n:n + 1])
                    P.tt("dve", ssq2[:ntok, :], ssq2h[:ntok, 0:1], ssq2h[:ntok, 1:2], ALU.add)
                    rstd_from_ssq(rstd2[:ntok, :], ssq2[:ntok, :], 1024, RMS_EPS)
                    P.stt(resid[ti][:ntok, :], resid[ti][:ntok, :], rstd2[:ntok, :], fgbc[:ntok, :], ALU.mult, ALU.mult)
                    if sample:
                        P.dma("sp", D["ys"].all(), resid[ti][:ntok, :])
                    else:
                        P.dma("sp", D["y"][ti * 128:(ti + 1) * 128, :], resid[ti].all())
                    yield 4.0
                if sample:
                    ufs, usTc = C["ufs"], C["usTc"]
                    for half in range(2):
                        pt = ring.get()
                        for cc in range(4):
                            c = half * 4 + cc
                            P.tr(pt[:16, cc * 128:(cc + 1) * 128], ufs[:, c, :], identf.all(), inc=(cc == 3))
                        P.copy("act", usTc[:, half * 512:(half + 1) * 512], pt[:16, :])
                    P.dma("sp", D["ccs"][:, 29, :], usTc.all())

            def work_bufs(esx, N, nb):
                return {"h1T": [P.sbuf("h1T%d" % i, [128, 8, N], BF16, esx) for i in range(nb)],
                        "yc": [P.sbuf("yc%d" % i, [128, 8, N], F32, esx) for i in range(nb)],
                        "sgb": [P.sbuf("sgb%d" % i, [128, N], F32, esx) for i in range(2)],
                        "yst": [P.sbuf("yst%d" % i, [128, 2, N], BF16, esx) for i in range(2)],
                        "msq": P.sbuf("msq", [128, N], F32, esx), "var": P.sbuf("var", [128, N], F32, esx),
                        "dd": [P.sbuf("dd%d" % i, [128, N], F32, esx) for i in range(2)],
                        "t2": [P.sbuf("t2%d" % i, [128, N], F32, esx) for i in range(2)],
                        "sl": [P.sbuf("sl%d" % i, [128, N], F32, esx) for i in range(2)],
                        "szc": [P.sbuf("szc%d" % i, [128, N], F32, esx) for i in range(2)],
                        "ycT": P.sbuf("ycT", [128, 8, N], BF16, esx)}

            MERGE_W = (1, 1)

            def mergeC(*gens):
                gens = [g for g in gens if g is not None]
                w = list(MERGE_W[:len(gens)]) if len(gens) > 1 else [1]
                while gens:
                    for gi, g in enumerate(list(gens)):
                        for _ in range(w[gi] if gi < len(w) else 1):
                            try:
                                next(g)
                            except StopIteration:
                                if g in gens:
                                    gens.remove(g)
                                break

            with ExitStack() as esCs:
                C = work_bufs(esCs, NS, 1)
                C.update({"sccb": P.sbuf("sccb", [120, 1024], F32, esCs), "bT": P.sbuf("bT", [128, 8, 480], BF16, esCs),
                          "ucs": P.sbuf("ucs", [128, 8, 16], BF16, esCs), "ufs": P.sbuf("ufs", [128, 8, 16], F32, esCs),
                          "usTc": P.sbuf("usTc", [16, 1024], F32, esCs)})
                mergeC(frontC(0, True, C))
                mergeC(backC(0, True, C))
                P.barrier()
            with ExitStack() as esCp:
                C = work_bufs(esCp, NT, 2)
                C["uc"] = [P.sbuf("uc%d" % c, [128, 30 + NT], BF16, esCp) for c in range(8)]
                for c in range(8):
                    P.memset("dve", C["uc"][c][:, 0:30], 0.0)
                nT = TOK // NT
                mergeC(frontC(0, False, C))
                for Tc in range(nT):
                    mergeC(frontC(Tc + 1, False, C) if Tc + 1 < nT else None, backC(Tc, False, C))
                P.barrier()
            with ExitStack() as esCe:
                ccpb = P.sbuf("ccpb", [30, 1024], F32, esCe)
                for half in range(2):
                    pt = ring.get()
                    for cc in range(4):
                        c = half * 4 + cc
                        P.tr(pt[:30, cc * 128:(cc + 1) * 128], ufl[:, c, :], identf.all(), inc=(cc == 3))
                    P.copy("act", ccpb[:, half * 512:(half + 1) * 512], pt[:30, :])
                P.dma("sp", D["ccp"].all(), ccpb.all())
                P.barrier()
        return _finish(nc, P, D, resid, debug_resid)


def _finish(nc, P, D, resid, debug_resid):
    if debug_resid:
        for i in range(17):
            P.dma("sp", D["dbg"][i], resid[i].all())
    P.barrier()
    print("ninstr", P.ninstr, "nsem", len(P.semobj))
    return nc


def _consts():
    identf = np.eye(128, dtype=np.float32)
    identb = identf.astype(ml_dtypes.bfloat16)
    trif = np.triu(np.ones((128, 128), dtype=np.float32))
    delta = np.broadcast_to(np.eye(16, dtype=np.float32)[None], (128, 16, 16)).copy()
    return {"c_identb": identb, "c_identf": identf, "c_trif": trif, "c_delta": delta}


def _fm(v, n):
    return np.ascontiguousarray(np.asarray(v, dtype=np.float32).reshape(n, 128).T)


def make_in_maps(inp):
    f = lambda k: np.asarray(inp[k], dtype=np.float32)
    shared = {
        "w_in_a": np.ascontiguousarray(f("w_in_a")[0]),
        "w_out_a": np.ascontiguousarray(f("w_out_a")[0]),
        "w_in_c": np.ascontiguousarray(f("w_in_c")[0]),
        "w_out_c": np.ascontiguousarray(f("w_out_c")[0]),
        "wgu": np.ascontiguousarray(np.concatenate([f("w_gate_up")[0], f("b_gate_up")[0][None, :]], axis=0)),
        "g0T": _fm(f("norm_g")[0], 8),
        "g1T": _fm(f("norm_g")[1], 8),
        "glaT": _fm(np.tile(f("gla_norm_g")[0], 4), 8),
        "wsT": np.ascontiguousarray(f("w_sconv")[0].T.reshape(8, 128, 3).transpose(1, 0, 2)),
        "wdT": np.ascontiguousarray(f("w_dwconv")[0].T.reshape(8, 128, 31).transpose(1, 0, 2)),
        "bdw": _fm(f("b_dwconv")[0], 8),
        "lng": _fm(f("ln_g")[0], 8),
        "lnb": _fm(f("ln_b")[0], 8),
        "binc": _fm(f("b_in_c")[0], 24),
        "bout": np.ascontiguousarray(f("b_out_c")[0][None, :]),
        "bincrow": np.ascontiguousarray(f("b_in_c")[0][None, :]),
        "fgbc": np.ascontiguousarray(np.broadcast_to(f("final_norm_g")[None, :], (128, 1024))),
    }
    shared.update(_consts())
    xp, xs = f("x_prompt"), f("x_sample")
    sg, ss, sc = f("state_gla"), f("state_sconv"), f("state_cconv")
    maps = []
    for b in range(NCORES):
        m = dict(shared)
        sl = slice(b * NS, (b + 1) * NS)
        m["x"] = np.ascontiguousarray(xp[b])
        m["xs"] = np.ascontiguousarray(xs[sl, 0, :])
        m["sgla"] = np.ascontiguousarray(sg[0, sl])
        m["ssc"] = np.ascontiguousarray(ss[0, sl])
        m["scc"] = np.ascontiguousarray(sc[0, sl])
        maps.append(m)
    return maps


_NC_CACHE = {}


def kernel(**inputs):
    if "nc" not in _NC_CACHE:
        _NC_CACHE["nc"] = build_nc()
    nc = _NC_CACHE["nc"]
    maps = make_in_maps(inputs)
    res = run_bass_kernel_spmd(nc, maps, core_ids=list(range(NCORES)))
    R = res.results
    y_prompt = np.stack([R[b]["y"] for b in range(NCORES)], axis=0)
    y_sample = np.concatenate([R[b]["ys"] for b in range(NCORES)], axis=0)[:, None, :]
    gla_p = np.stack([R[b]["glap"] for b in range(NCORES)], axis=0)[None]
    sconv_p = np.stack([R[b]["scp"] for b in range(NCORES)], axis=0)[None]
    cconv_p = np.stack([R[b]["ccp"] for b in range(NCORES)], axis=0)[None]
    gla_s = np.concatenate([R[b]["glas"] for b in range(NCORES)], axis=0)[None]
    sconv_s = np.concatenate([R[b]["scs"] for b in range(NCORES)], axis=0)[None]
    cconv_s = np.concatenate([R[b]["ccs"] for b in range(NCORES)], axis=0)[None]
    outs = (y_prompt, y_sample, gla_p, sconv_p, cconv_p, gla_s, sconv_s, cconv_s)
    return tuple(np.ascontiguousarray(o, dtype=np.float32) for o in outs)
```
